# Optimizing a Trainium2 kernel written in Bass

```python
import math
import jax, jax.numpy as jnp
from jax import lax
import numpy as np

D_MODEL = 1024
BATCH = 16
SEQ = 256
DEPTH = 4
DEC_BATCH = 2
DEC_SEQ = 1024
PAST_LEN = 256

GRID_W = 64
MIX_W = D_MODEL
RET_HEADS = 4
RET_DH = 96
RET_W = RET_HEADS * RET_DH
HY_W = 256
HY_ORDER = 2
ML_HEADS = 4
ML_DH = 96
ML_W = ML_HEADS * ML_DH
IN_W = 4 * RET_W + 3 * HY_W + 4 * ML_W + 2 * 2 * ML_HEADS
SPLIT_POINTS = (4 * RET_W, 4 * RET_W + 3 * HY_W, 4 * RET_W + 3 * HY_W + 4 * ML_W)
D_FF = 4 * D_MODEL
CHUNK = 128
N_BANDS = 16
FEAT_W = 1 + 2 * N_BANDS
FILT_W = 64
HY_SHIFT = 0.05
HY_TARGET = 1e-2
HY_SHORT_DECAY_PCT = 0.3
HY_LONG_DECAY_PCT = 1.5
ROPE_BASE = 10000.0
EPS = 1e-6

kernel_name = "hybrid_diffusion_trunk_step"


def rmsnorm(x, g):
    xf = x.astype(jnp.float32)
    return xf * lax.rsqrt(jnp.mean(xf * xf, axis=-1, keepdims=True) + EPS) * g


def head_norm(x, g):
    xn = x * lax.rsqrt(jnp.mean(x * x, axis=-1, keepdims=True) + EPS)
    return xn.reshape(x.shape[:2] + (-1,)) * g


def dwconv3(x, w, b):
    xpad = jnp.pad(x, ((0, 0), (1, 1), (0, 0)))
    return xpad[:, :-2] * w[0] + xpad[:, 1:-1] * w[1] + xpad[:, 2:] * w[2] + b


def rope_2d(x):
    L, dh = x.shape[1], x.shape[-1]
    rows = L // GRID_W
    row = jnp.repeat(jnp.arange(rows, dtype=jnp.float32), GRID_W)
    col = jnp.tile(jnp.arange(GRID_W, dtype=jnp.float32), rows)
    half = dh // 2
    n_freq = half // 2
    freqs = ROPE_BASE ** (-jnp.arange(n_freq, dtype=jnp.float32) / n_freq)
    ang = jnp.concatenate([row[:, None] * freqs, col[:, None] * freqs], axis=-1)
    cos = jnp.cos(ang)[None, :, None, :]
    sin = jnp.sin(ang)[None, :, None, :]
    x1, x2 = x[..., :half], x[..., half:]
    return jnp.concatenate([x1 * cos - x2 * sin, x2 * cos + x1 * sin], axis=-1)


def _chunks(x):
    B, L, H = x.shape[:3]
    xr = x.reshape((B, L // CHUNK, CHUNK, H) + x.shape[3:])
    return jnp.moveaxis(xr, (1, 3), (0, 2))


def _unchunks(y):
    yr = jnp.moveaxis(y, (0, 2), (1, 3))
    n, c = yr.shape[1], yr.shape[2]
    return yr.reshape((yr.shape[0], n * c) + yr.shape[3:])


def retention_chunked(q, k, v, log_gamma, s0):
    idx = jnp.arange(CHUNK, dtype=jnp.float32)
    rel = idx[:, None] - idx[None, :]
    causal = rel >= 0
    decay_mask = jnp.where(causal[None], jnp.exp(jnp.maximum(rel, 0.0)[None] * log_gamma[:, None, None]), 0.0)
    q_decay = jnp.exp((idx + 1.0)[None, :] * log_gamma[:, None])
    k_decay = jnp.exp((CHUNK - 1.0 - idx)[None, :] * log_gamma[:, None])
    chunk_decay = jnp.exp(CHUNK * log_gamma)

    def step(s, inp):
        qb, kb, vb = inp
        scores = jnp.einsum('bhid,bhjd->bhij', qb, kb) * decay_mask
        intra = jnp.einsum('bhij,bhjd->bhid', scores, vb)
        inter = jnp.einsum('bhid,bhde->bhie', qb, s) * q_decay[None, :, :, None]
        s_new = s * chunk_decay[None, :, None, None] + jnp.einsum('bhjd,bhje->bhde', kb * k_decay[None, :, :, None], vb)
        return s_new, intra + inter

    s_fin, out = lax.scan(step, s0, (_chunks(q), _chunks(k), _chunks(v)))
    return _unchunks(out), s_fin


def mlstm_chunked(q, k, v, i_pre, log_f, c0, n0, m0):
    idx = jnp.arange(CHUNK)
    causal = idx[:, None] >= idx[None, :]

    def step(carry, inp):
        c, nv, m = carry
        qb, kb, vb, ib, fb = inp
        b = jnp.cumsum(fb, axis=-1)
        d_log = jnp.where(causal, b[..., :, None] - b[..., None, :] + ib[..., None, :], -jnp.inf)
        a = b + m[..., None]
        m_t = jnp.maximum(a, jnp.max(d_log, axis=-1))
        w_intra = jnp.exp(d_log - m_t[..., None])
        w_inter = jnp.exp(a - m_t)
        s = jnp.einsum('bhid,bhjd->bhij', qb, kb) * w_intra
        num = jnp.einsum('bhij,bhjd->bhid', s, vb) + w_inter[..., None] * jnp.einsum('bhid,bhde->bhie', qb, c)
        den = jnp.sum(s, axis=-1) + w_inter * jnp.einsum('bhid,bhd->bhi', qb, nv)
        h = num / jnp.maximum(jnp.abs(den), jnp.exp(-m_t))[..., None]
        b_last = b[..., -1]
        g_log = b_last[..., None] - b + ib
        m_new = jnp.maximum(b_last + m, jnp.max(g_log, axis=-1))
        w_prev = jnp.exp(b_last + m - m_new)
        kw = kb * jnp.exp(g_log - m_new[..., None])[..., None]
        c_new = w_prev[..., None, None] * c + jnp.einsum('bhjd,bhje->bhde', kw, vb)
        n_new = w_prev[..., None] * nv + jnp.sum(kw, axis=-2)
        return (c_new, n_new, m_new), h

    fin, out = lax.scan(step, (c0, n0, m0), (_chunks(q), _chunks(k), _chunks(v), _chunks(i_pre), _chunks(log_f)))
    return _unchunks(out), fin


def hyena_filter_spectra(L, lp):
    tn = jnp.arange(L, dtype=jnp.float32) / L
    bands = jnp.linspace(1e-4, N_BANDS - 1, N_BANDS, dtype=jnp.float32)
    ang = 2.0 * math.pi * tn[:, None] * bands[None, :]
    feat = jnp.concatenate([tn[:, None], jnp.cos(ang), jnp.sin(ang)], axis=-1)
    fr = lp['hy_sin_freq']
    hdn = jnp.sin(fr * (feat @ lp['hy_f_w1'] + lp['hy_f_b1']))
    hdn = jnp.sin(fr * (hdn @ lp['hy_f_w2'] + lp['hy_f_b2']))
    filt = (hdn @ lp['hy_f_w3'] + lp['hy_f_b3']).astype(jnp.float32).reshape(L, 2, HY_ORDER, HY_W)
    deltas = jnp.abs(jnp.linspace(math.log(HY_TARGET) / HY_LONG_DECAY_PCT,
                                  math.log(HY_TARGET) / HY_SHORT_DECAY_PCT, HY_W, dtype=jnp.float32))
    window = jnp.exp(-tn[:, None] * deltas[None, :]) + HY_SHIFT
    filt = filt * window[:, None, None, :]
    fwd, bwd = filt[:, 0], filt[:, 1]
    full = jnp.concatenate([fwd, jnp.zeros((1, HY_ORDER, HY_W), jnp.float32), bwd[:0:-1]], axis=0)
    return jnp.fft.rfft(full, axis=0)


def long_conv(z, kf):
    L = z.shape[1]
    zf = jnp.fft.rfft(z, n=2 * L, axis=1)
    return jnp.fft.irfft(zf * kf[None], n=2 * L, axis=1)[:, :L]


def mixer(h, lp, init, latent):
    B, L, _ = h.shape
    s_ret0, c0, n0, m0 = [t.astype(jnp.float32) for t in init]
    proj = jnp.einsum('bld,de->ble', h, lp['w_in']).astype(jnp.float32)
    r_part, hy_part, ml_part, g_part = jnp.split(proj, SPLIT_POINTS, axis=-1)

    rq, rk, rv, rg = jnp.split(r_part, 4, axis=-1)
    rq = rq.reshape(B, L, RET_HEADS, RET_DH)
    rk = rk.reshape(B, L, RET_HEADS, RET_DH)
    rv = rv.reshape(B, L, RET_HEADS, RET_DH)
    if latent:
        rq, rk = rope_2d(rq), rope_2d(rk)
    rk = rk * RET_DH ** -0.5
    log_gamma = jax.nn.log_sigmoid(lp['ret_decay_logit'].astype(jnp.float32))
    o_f, s_f = retention_chunked(rq, rk, rv, log_gamma[0], s_ret0[:, 0])
    o_b, s_b = retention_chunked(rq[:, ::-1], rk[:, ::-1], rv[:, ::-1], log_gamma[1], s_ret0[:, 1])
    ret = head_norm(o_f + o_b[:, ::-1], lp['ret_norm_g']) * jax.nn.silu(rg)

    hy = dwconv3(hy_part, lp['hy_conv_w'], lp['hy_conv_b'])
    hv, hx1, hx2 = jnp.split(hy, 3, axis=-1)
    kf = hyena_filter_spectra(L, lp)
    z = hx1 * (long_conv(hv, kf[:, 0]) + lp['hy_bias'][0] * hv)
    hy_out = hx2 * (long_conv(z, kf[:, 1]) + lp['hy_bias'][1] * z)

    mq, mk, mv, mo = jnp.split(ml_part, 4, axis=-1)
    mq = mq.reshape(B, L, ML_HEADS, ML_DH)
    mk = mk.reshape(B, L, ML_HEADS, ML_DH) * ML_DH ** -0.5
    mv = mv.reshape(B, L, ML_HEADS, ML_DH)
    gates = g_part.reshape(B, L, 2, 2, ML_HEADS) + lp['ml_gate_bias']
    i_pre = gates[:, :, :, 0]
    log_f = jax.nn.log_sigmoid(gates[:, :, :, 1])
    h_f, st_f = mlstm_chunked(mq, mk, mv, i_pre[:, :, 0], log_f[:, :, 0], c0[:, 0], n0[:, 0], m0[:, 0])
    h_b, st_b = mlstm_chunked(mq[:, ::-1], mk[:, ::-1], mv[:, ::-1], i_pre[:, ::-1, 1], log_f[:, ::-1, 1],
                              c0[:, 1], n0[:, 1], m0[:, 1])
    ml = head_norm(h_f + h_b[:, ::-1], lp['ml_norm_g']) * jax.nn.sigmoid(mo)

    mixed = jnp.concatenate([ret, hy_out, ml], axis=-1)
    out = jnp.einsum('ble,ed->bld', mixed, lp['w_out'])
    states = (jnp.stack([s_f, s_b], axis=1), jnp.stack([st_f[0], st_b[0]], axis=1),
              jnp.stack([st_f[1], st_b[1]], axis=1), jnp.stack([st_f[2], st_b[2]], axis=1))
    return out, states


def conv_ffn(h, lp):
    up = jnp.einsum('bld,df->blf', h, lp['w_up'])
    a, b = jnp.split(up, 2, axis=-1)
    a = dwconv3(a, lp['ffn_conv_w'], lp['ffn_conv_b'])
    return jnp.einsum('blf,fd->bld', jax.nn.gelu(a, approximate=True) * b, lp['w_down'])


def trunk_layer(x, cvec, lp, init, latent):
    mod = jax.nn.silu(cvec.astype(jnp.float32)) @ lp['w_mod'] + lp['b_mod']
    sh1, sc1, g1, sh2, sc2, g2 = jnp.split(mod[:, None, :], 6, axis=-1)
    h = rmsnorm(x, lp['norm_mix_pre']) * (1.0 + sc1) + sh1
    mix, states = mixer(h, lp, init, latent)
    x1 = x + g1 * rmsnorm(mix, lp['norm_mix_post'])
    h = rmsnorm(x1, lp['norm_ffn_pre']) * (1.0 + sc2) + sh2
    out = x1 + g2 * rmsnorm(conv_ffn(h, lp), lp['norm_ffn_post'])
    return out.astype(x.dtype), states


def setup_inputs(seed: int = 0) -> dict:
    key = jax.random.key(seed)
    keys = jax.random.split(key, 48)
    counter = [0]

    def nrm(shape, scale):
        k = keys[counter[0]]
        counter[0] += 1
        return jax.random.normal(k, shape, jnp.float32) * scale

    D = D_MODEL
    gain = lambda shape: 1.0 + nrm(shape, 0.05)
    ret_logit = jnp.log(2.0 ** (5.0 + jnp.arange(RET_HEADS, dtype=jnp.float32)) - 1.0)
    gate_base = jnp.stack([jnp.zeros((ML_HEADS,), jnp.float32), jnp.linspace(3.0, 6.0, ML_HEADS, dtype=jnp.float32)])
    return {
        'x_prompt': nrm((BATCH, SEQ, D), 1.0),
        'x_sample': nrm((DEC_BATCH, DEC_SEQ, D), 1.0),
        'c': nrm((DEC_BATCH, D), 1.0),
        'state_ret': nrm((DEC_BATCH, DEPTH, 2, RET_HEADS, RET_DH, RET_DH), 0.5),
        'state_mlstm_c': nrm((DEC_BATCH, DEPTH, 2, ML_HEADS, ML_DH, ML_DH), 0.5),
        'state_mlstm_n': nrm((DEC_BATCH, DEPTH, 2, ML_HEADS, ML_DH), 0.5),
        'state_mlstm_m': nrm((DEC_BATCH, DEPTH, 2, ML_HEADS), 0.5),
        'c_ctx': nrm((D,), 1.0),
        'norm_mix_pre': gain((DEPTH, D)),
        'norm_mix_post': gain((DEPTH, D)),
        'norm_ffn_pre': gain((DEPTH, D)),
        'norm_ffn_post': gain((DEPTH, D)),
        'w_mod': nrm((DEPTH, D, 6 * D), 0.3 * D ** -0.5),
        'b_mod': nrm((DEPTH, 6 * D), 0.02),
        'w_in': nrm((DEPTH, D, IN_W), D ** -0.5),
        'w_out': nrm((DEPTH, MIX_W, D), MIX_W ** -0.5),
        'ret_decay_logit': jnp.broadcast_to(ret_logit, (DEPTH, 2, RET_HEADS)) + nrm((DEPTH, 2, RET_HEADS), 0.1),
        'ret_norm_g': gain((DEPTH, RET_W)),
        'hy_conv_w': nrm((DEPTH, 3, 3 * HY_W), 0.5),
        'hy_conv_b': nrm((DEPTH, 3 * HY_W), 0.02),
        'hy_f_w1': nrm((DEPTH, FEAT_W, FILT_W), FEAT_W ** -0.5),
        'hy_f_b1': nrm((DEPTH, FILT_W), 0.1),
        'hy_f_w2': nrm((DEPTH, FILT_W, FILT_W), FILT_W ** -0.5),
        'hy_f_b2': nrm((DEPTH, FILT_W), 0.1),
        'hy_f_w3': nrm((DEPTH, FILT_W, 2 * HY_ORDER * HY_W), 0.1 * FILT_W ** -0.5),
        'hy_f_b3': nrm((DEPTH, 2 * HY_ORDER * HY_W), 0.01),
        'hy_sin_freq': 1.0 + nrm((DEPTH, FILT_W), 0.1),
        'hy_bias': nrm((DEPTH, HY_ORDER, HY_W), 0.5),
        'ml_gate_bias': jnp.broadcast_to(gate_base, (DEPTH, 2, 2, ML_HEADS)) + nrm((DEPTH, 2, 2, ML_HEADS), 0.1),
        'ml_norm_g': gain((DEPTH, ML_W)),
        'w_up': nrm((DEPTH, D, 2 * D_FF), D ** -0.5),
        'ffn_conv_w': nrm((DEPTH, 3, D_FF), 0.5),
        'ffn_conv_b': nrm((DEPTH, D_FF), 0.02),
        'w_down': nrm((DEPTH, D_FF, D), D_FF ** -0.5),
    }


def reference(x_prompt, x_sample, c, state_ret, state_mlstm_c, state_mlstm_n, state_mlstm_m, c_ctx,
              norm_mix_pre, norm_mix_post, norm_ffn_pre, norm_ffn_post, w_mod, b_mod, w_in, w_out,
              ret_decay_logit, ret_norm_g, hy_conv_w, hy_conv_b, hy_f_w1, hy_f_b1, hy_f_w2, hy_f_b2,
              hy_f_w3, hy_f_b3, hy_sin_freq, hy_bias, ml_gate_bias, ml_norm_g,
              w_up, ffn_conv_w, ffn_conv_b, w_down):
    f32 = jnp.float32
    bp = x_prompt.shape[0]
    c_prompt = jnp.broadcast_to(c_ctx[None, :], (bp, D_MODEL))
    zero_init = (jnp.zeros((bp, 2, RET_HEADS, RET_DH, RET_DH), f32),
                 jnp.zeros((bp, 2, ML_HEADS, ML_DH, ML_DH), f32),
                 jnp.zeros((bp, 2, ML_HEADS, ML_DH), f32),
                 jnp.zeros((bp, 2, ML_HEADS), f32))
    xp, xs = x_prompt, x_sample
    new_ret, new_c, new_n, new_m = [], [], [], []
    for l in range(DEPTH):
        lp = {
            'norm_mix_pre': norm_mix_pre[l], 'norm_mix_post': norm_mix_post[l],
            'norm_ffn_pre': norm_ffn_pre[l], 'norm_ffn_post': norm_ffn_post[l],
            'w_mod': w_mod[l], 'b_mod': b_mod[l], 'w_in': w_in[l], 'w_out': w_out[l],
            'ret_decay_logit': ret_decay_logit[l], 'ret_norm_g': ret_norm_g[l],
            'hy_conv_w': hy_conv_w[l], 'hy_conv_b': hy_conv_b[l],
            'hy_f_w1': hy_f_w1[l], 'hy_f_b1': hy_f_b1[l], 'hy_f_w2': hy_f_w2[l], 'hy_f_b2': hy_f_b2[l],
            'hy_f_w3': hy_f_w3[l], 'hy_f_b3': hy_f_b3[l], 'hy_sin_freq': hy_sin_freq[l], 'hy_bias': hy_bias[l],
            'ml_gate_bias': ml_gate_bias[l], 'ml_norm_g': ml_norm_g[l],
            'w_up': w_up[l], 'ffn_conv_w': ffn_conv_w[l], 'ffn_conv_b': ffn_conv_b[l], 'w_down': w_down[l],
        }
        xp, st = trunk_layer(xp, c_prompt, lp, zero_init, False)
        new_ret.append(st[0])
        new_c.append(st[1])
        new_n.append(st[2])
        new_m.append(st[3])
        cache_init = (state_ret[:, l], state_mlstm_c[:, l], state_mlstm_n[:, l], state_mlstm_m[:, l])
        xs, _ = trunk_layer(xs, c, lp, cache_init, True)
    return (xp, xs, jnp.stack(new_ret, axis=1), jnp.stack(new_c, axis=1), jnp.stack(new_n, axis=1), jnp.stack(new_m, axis=1))
```

```python
from contextlib import ExitStack
import math
import numpy as np
import concourse.bass as bass
import concourse.mybir as mybir
from concourse.bass_utils import run_bass_kernel_spmd

F32 = mybir.dt.float32
BF16 = mybir.dt.bfloat16
AF = mybir.ActivationFunctionType
ALU = mybir.AluOpType
AX = mybir.AxisListType

D = 1024
T = 1024
DEPTH = 4
NCH = 8
DH = 96
EPS = 1e-6
SCL = DH ** -0.5
ENGS = ("pe", "act", "dve", "pool", "sp")
SAME_ENG_MIN = 1 << 30


class Region:
    __slots__ = ("last_w", "readers")

    def __init__(self):
        self.last_w = None
        self.readers = []


class V:
    def __init__(self, ap, regs):
        self.ap = ap
        self.regs = regs

    def __getitem__(self, idx):
        return self.ap[idx]


class DSem:
    def __init__(self, h):
        self.h = h
        self.count = 0


class Op:
    __slots__ = ("eng", "fn", "deps", "dsem", "dval", "sig", "need_sig", "idx", "line", "osz")

    def __init__(self, eng, fn, deps, dsem):
        self.eng = eng
        self.fn = fn
        self.deps = deps
        self.dsem = dsem
        self.dval = None
        self.sig = None
        self.need_sig = False


def _prod(s):
    n = 1
    for a in s:
        n *= a
    return n


class Prog:
    def __init__(self, nc):
        self.nc = nc
        self.es = ExitStack()
        self.ops = []
        self.sb_bytes = 0

    def sb(self, name, shape, dt, nreg=1):
        t = self.es.enter_context(self.nc.sbuf_tensor("s_" + name, list(shape), dt))
        self.sb_bytes += _prod(shape[1:]) * (4 if dt == F32 else 2)
        return V(t[:] if True else t, [Region() for _ in range(nreg)])

    def ps(self, name, shape, dt=F32, nreg=1):
        t = self.es.enter_context(self.nc.psum_tensor("p_" + name, list(shape), dt))
        return V(t[:], [Region() for _ in range(nreg)])

    def dsem(self, name):
        return DSem(self.es.enter_context(self.nc.semaphore(name)))

    def _regs(self, lst):
        out = []
        for x in lst:
            if x is None:
                continue
            if isinstance(x, Region):
                out.append(x)
            elif hasattr(x, "regs"):
                out.extend(x.regs)
            else:
                out.extend(self._regs(x))
        return out

    def op(self, eng, fn, reads=(), writes=(), dsem=None, osz=0):
        rr = self._regs(reads)
        ww = self._regs(writes)
        deps = set()
        for r in rr:
            if r.last_w is not None:
                deps.add(r.last_w)
        for w in ww:
            if w.last_w is not None:
                deps.add(w.last_w)
            deps.update(w.readers)
        o = Op(eng, fn, deps, dsem)
        o.osz = osz
        o.idx = len(self.ops)
        import sys as _s
        fr = _s._getframe(1)
        while fr.f_code.co_name in ("mm", "tr", "act", "tt", "ts", "stt", "cp", "memset", "load", "store", "op"):
            fr = fr.f_back
        o.line = fr.f_lineno
        if dsem is not None:
            dsem.count += 1
            o.dval = 16 * dsem.count
        self.ops.append(o)
        wset = set(id(w) for w in ww)
        for w in ww:
            w.last_w = o
            w.readers = []
        for r in rr:
            if id(r) not in wset:
                if dsem is None:
                    r.readers = [q for q in r.readers if q.dsem is not None or q.eng != eng]
                r.readers.append(o)
        return o

    def emit(self, final_dsems=()):
        nc = self.nc

        def needs(d, o):
            if d.dsem is not None:
                return True
            if d.eng != o.eng:
                return True
            if o.eng == "pe":
                return False
            return d.osz < SAME_ENG_MIN

        for o in self.ops:
            for d in o.deps:
                if d.dsem is None and needs(d, o):
                    d.need_sig = True
        EPOCH = 1800
        cnt = {e: 0 for e in ENGS}
        for o in self.ops:
            if o.dsem is None and o.need_sig:
                o.sig = (cnt[o.eng] // EPOCH, cnt[o.eng] % EPOCH + 1)
                cnt[o.eng] += 1
        self.sig_counts = cnt
        esem = {e: [self.es.enter_context(nc.semaphore(f"es_{e}{i}")) for i in range(cnt[e] // EPOCH + 1)]
                for e in ENGS}
        per = {e: [o for o in self.ops if o.eng == e] for e in ENGS}
        handles = {"pe": "tensor", "act": "scalar", "dve": "vector", "pool": "gpsimd", "sp": "sync"}
        with nc.Block() as block:
            for e in ENGS:

                def body(eng, lst=per[e], e=e):
                    known = {}
                    for o in lst:
                        waits = {}
                        for d in o.deps:
                            if not needs(d, o):
                                continue
                            if d.dsem is not None:
                                key, h, v = ("d", id(d.dsem)), d.dsem.h, (0, d.dval)
                            else:
                                key, h, v = ("e", d.eng), None, d.sig
                            if known.get(key, (0, 0)) >= v:
                                continue
                            if key not in waits or waits[key][1] < v:
                                waits[key] = (h, v)
                        for key, (h, v) in waits.items():
                            if h is None:
                                h = esem[key[1]][v[0]]
                            eng.wait_ge(h, v[1])
                            known[key] = v
                        ins = o.fn(eng)
                        if o.dsem is not None:
                            ins.then_inc(o.dsem.h, 16)
                        elif o.need_sig:
                            ins.then_inc(esem[e][o.sig[0]], 1)
                    if e == "sp":
                        for d in final_dsems:
                            if d.count > 0:
                                eng.wait_ge(d.h, 16 * d.count)

                getattr(block, handles[e])(body)


class Arena:
    GR = 256

    def __init__(self, P, name, nbytes):
        self.nbytes = nbytes
        self.v = P.sb(name, [128, nbytes // 4], F32, nreg=nbytes // self.GR)
        self.off = 0
        self.peak = 0

    def alloc(self, shape, dt):
        n = _prod(shape[1:])
        nb = n * (2 if dt == BF16 else 4)
        nb_al = -(-nb // self.GR) * self.GR
        off = self.off
        self.off += nb_al
        self.peak = max(self.peak, self.off)
        assert self.off <= self.nbytes, f"arena overflow {self.off} > {self.nbytes}"
        ap = self.v.ap[0:shape[0], off // 4: (off + nb_al) // 4]
        if dt == BF16:
            ap = ap.bitcast(BF16)
        ap = ap[:, 0:n]
        if len(shape) == 3:
            ap = ap.rearrange("p (a b) -> p a b", a=shape[1])
        elif len(shape) == 4:
            ap = ap.rearrange("p (a b c) -> p a b c", a=shape[1], b=shape[2])
        return V(ap, self.v.regs[off // self.GR: (off + nb_al) // self.GR])

    def mark(self):
        return self.off

    def reset(self, to=0):
        self.off = to


NLP = 238
NLR = 792
NVEC = 18
NCT = 6 * 128 + 8
NHM = 64 + 64 + 1024 + 3


def build(dbg=None, nlayers=DEPTH, stop=None, wdepth=DEPTH):
    nc = bass.Bass("TRN2", target_bir_lowering=False)
    P = Prog(nc)

    def din(name, shape):
        return nc.dram_tensor(name, list(shape), F32, kind="ExternalInput").ap()

    def dout(name, shape):
        return nc.dram_tensor(name, list(shape), F32, kind="ExternalOutput").ap()

    xT_d = din("xT", [D, T])
    cfm_d = din("cfm", [128, 8])
    vecs_d = din("vecs", [128, NVEC])
    rope_d = din("rope", [128, 2, NCH, DH])
    dftc_d = din("dftC", [T, T])
    dfts_d = din("dftS", [T, T])
    ctab_d = din("ctab", [128, NCT])
    hfeat_d = din("hfeat", [33, 2, T])
    hsm_d = din("hsm", [128, 40 + 256])
    hsvd_d = din("hsvd", [128, 16])
    lp_d = din("lp", [DEPTH, 128, NLP])
    lrep_d = din("lrep", [DEPTH, 128, NLR])
    hmlp_d = din("hmlp", [DEPTH, 65, NHM])
    hyb_d = din("hyb", [DEPTH, 128, 512])
    sret_d = din("sret", [DEPTH, DH, 2, 4, DH])
    scn_d = din("scn", [DEPTH, DH, 2, 4, DH + 1])
    smm_d = din("smm", [DEPTH, 128, 8])
    wmod_d = din("w_mod", [wdepth, D, 6 * D])
    wintm_d = din("w_in_tm", [wdepth, D, 3088])
    winfm_d = din("w_in_fm", [wdepth, D, 768])
    wout_d = din("w_out", [wdepth, D, D])
    wup_d = din("w_up", [wdepth, D, 8 * D])
    wdown_d = din("w_down", [wdepth, 4 * D, D])

    yT_d = dout("yT", [D, T])
    oret_d = dout("o_ret", [4, DEPTH, 2, 4, DH, DH])
    oc_d = dout("o_c", [4, DEPTH, 2, 4, DH, DH])
    on_d = dout("o_n", [4, DEPTH, 2, 4, DH])
    om_d = dout("o_m", [4, DEPTH, 2, 4])

    def mm(out, lhsT, rhs, start, stop, reads, writes):
        P.op("pe", lambda e: e.matmul(out, lhsT, rhs, start=start, stop=stop), reads=reads, writes=writes)

    def tr(out, in_, ident, reads, writes):
        P.op("pe", lambda e: e.transpose(out, in_, ident), reads=reads, writes=writes)

    def act(out, in_, func, reads, writes, scale=None, bias=None):
        kw = {}
        if scale is not None:
            kw["scale"] = scale
        if bias is not None:
            kw["bias"] = bias
        P.op("act", lambda e: e.activation(out=out, in_=in_, func=func, **kw), reads=reads, writes=writes,
             osz=_prod(out.shape[1:]))

    def tt(eng, out, in0, in1, op, reads, writes):
        P.op(eng, lambda e: e.tensor_tensor(out=out, in0=in0, in1=in1, op=op), reads=reads, writes=writes,
             osz=_prod(out.shape[1:]))

    def ts(eng, out, in0, s1, s2, op0, op1, reads, writes):
        if op1 is None:
            P.op(eng, lambda e: e.tensor_scalar(out=out, in0=in0, scalar1=s1, scalar2=None, op0=op0),
                 reads=reads, writes=writes, osz=_prod(out.shape[1:]))
        else:
            P.op(eng, lambda e: e.tensor_scalar(out=out, in0=in0, scalar1=s1, scalar2=s2, op0=op0, op1=op1),
                 reads=reads, writes=writes, osz=_prod(out.shape[1:]))

    def stt(out, in0, scalar, in1, op0, op1, reads, writes):
        P.op("dve", lambda e: e.scalar_tensor_tensor(out=out, in0=in0, scalar=scalar, in1=in1, op0=op0, op1=op1),
             reads=reads, writes=writes, osz=_prod(out.shape[1:]))

    def cp(eng, out, in_, reads, writes):
        if eng == "act":
            P.op("act", lambda e: e.copy(out=out, in_=in_), reads=reads, writes=writes, osz=_prod(out.shape[1:]))
        else:
            P.op(eng, lambda e: e.tensor_copy(out=out, in_=in_), reads=reads, writes=writes,
                 osz=_prod(out.shape[1:]))

    def memset(eng, ap, val, writes):
        P.op(eng, lambda e: e.memset(ap, val), writes=writes)

    ld_sems = {}
    st_sems = {}

    def _sem_for(table, v, pfx):
        k = id(v.regs[0])
        if k not in table:
            table[k] = P.dsem(f"{pfx}{len(table)}")
        return table[k]

    def load(eng, out_v, out_ap, in_ap):
        P.op(eng, lambda e: e.dma_start(out=out_ap, in_=in_ap), writes=[out_v], dsem=_sem_for(ld_sems, out_v, "ld"))

    def store(out_ap, in_v, in_ap, slow=False, eng="sp"):
        ds = _sem_for(st_sems, in_v, "st")
        if slow:
            P.op(eng, lambda e: e.dma_start(out=out_ap, in_=in_ap, allow_slow_non_contiguous=True), reads=[in_v],
                 dsem=ds)
        else:
            P.op(eng, lambda e: e.dma_start(out=out_ap, in_=in_ap), reads=[in_v], dsem=ds)

    dbg_outs = []

    def dump(name, v, ap, shape):
        if dbg is None or name not in dbg:
            return
        o = dout("dbg_" + name, shape)
        dbg_outs.append("dbg_" + name)
        idx = tuple(slice(None) for _ in shape)
        store(o[idx], v, ap, eng=("sp" if ap.dtype == F32 else "pool"))

    pst = [P.ps(f"psum{i}", [128, 1024], F32, nreg=2) for i in range(4)]
    bank_i = [0]
    pair_i = [0]

    held = set()

    def bank():
        while True:
            i = bank_i[0] % 8
            bank_i[0] += 1
            t = pst[i // 2]
            h = i % 2
            if id(t.regs[h]) not in held:
                return V(t.ap[:, h * 512:(h + 1) * 512], [t.regs[h]])

    def hold(v):
        held.add(id(v.regs[0]))

    def release(v):
        held.discard(id(v.regs[0]))

    def pair():
        i = pair_i[0] % 4
        pair_i[0] += 1
        return pst[i]

    x = [P.sb(f"x{k}", [128, T], F32) for k in range(8)]
    hb = [P.sb(f"h{k}", [128, T], BF16) for k in range(8)]
    mixT = P.sb("mixT", [128, 8, T], BF16, nreg=8)

    def mixr(k0, k1):
        return V(None, mixT.regs[k0:k1])

    NSLOT = 6
    slots = [P.sb(f"ring{i}", [128, 8, 256], BF16) for i in range(NSLOT)]
    slot_sem = [P.dsem(f"rs{i}") for i in range(NSLOT)]
    ring_i = [0]

    def ring_load(view, kt, ncols):
        i = ring_i[0] % NSLOT
        ring_i[0] += 1
        s = slots[i]
        P.op("pool", lambda e: e.dma_start(out=s.ap[:, 0:kt, 0:ncols], in_=view), writes=[s], dsem=slot_sem[i])
        return s

    Cm = P.sb("Cm", [128, 8, T], BF16)
    Sm = P.sb("Sm", [128, 8, T], BF16)
    ctab = P.sb("ctab", [128, NCT], F32)
    RP, RN = ctab.ap[:, 0:128], ctab.ap[:, 128:256]
    MLE, MGE = ctab.ap[:, 256:384], ctab.ap[:, 384:512]
    IDX1, IDXR = ctab.ap[:, 512:640], ctab.ap[:, 640:768]
    JREV4, J4 = ctab.ap[:, 768:772], ctab.ap[:, 772:776]
    identf = P.sb("identf", [128, 128], F32)
    identb = P.sb("identb", [128, 128], BF16)
    onesb = P.sb("onesb", [128, 128], BF16)
    onesf = P.sb("onesf", [128, 128], F32)
    rope = P.sb("rope", [128, 2, NCH, DH], F32)
    vecs = P.sb("vecs", [128, NVEC], F32)
    cfm = P.sb("cfm", [128, 8], F32)
    cs = P.sb("cs", [128, 8], BF16)
    hsm = P.sb("hsm", [128, 40 + 256], F32)
    HCOS, HSIN, HNSIN = hsm.ap[:, 0:8], hsm.ap[:, 8:16], hsm.ap[:, 16:24]

    def HTN(d, tt_):
        return hsm.ap[:, 24 + d * 8 + tt_: 25 + d * 8 + tt_]

    hsv = P.sb("hsv", [128, 16], F32)
    hsv05 = P.sb("hsv05", [128, 16], F32)
    hnd = V(hsm.ap[:, 40:296], hsm.regs)
    lp = P.sb("lp", [128, NLP], F32)
    lrep = P.sb("lrep", [128, NLR], F32)
    hmlp = P.sb("hmlp", [65, NHM], F32)
    w3b = P.sb("w3b", [65, 1024], BF16)
    sret = P.sb("sret", [DH, 2, 4, DH], F32)
    scn = P.sb("scn", [DH, 2, 4, DH + 1], F32)
    smm = P.sb("smm", [128, 8], F32)
    modt = P.sb("modt", [128, 48], F32)
    modraw = [P.sb(f"modraw{i}", [128, 48], F32) for i in range(2)]
    gmt = P.sb("gmt", [128, 32], F32)
    cvx = P.sb("cvx", [128, 2 * 38], F32)
    lgt = P.sb("lgt", [128, 8], F32)
    dmask = P.sb("dmask", [128, 4, 128], BF16)
    qdec = P.sb("qdec", [DH, 2, 4, 128], F32)
    kdec = P.sb("kdec", [128, 8], F32)
    cdec = P.sb("cdec", [DH, 8], F32)
    frs = P.sb("frs", [64, 4], F32)

    arena = Arena(P, "arena", 48 * 1024)

    for k in range(8):
        load("sp", x[k], x[k].ap, xT_d[k * 128:(k + 1) * 128, :])
    load("sp", ctab, ctab.ap, ctab_d[:, :])
    load("sp", rope, rope.ap, rope_d[:, :, :, :])
    load("sp", vecs, vecs.ap, vecs_d[:, :])
    load("sp", cfm, cfm.ap, cfm_d[:, :])
    load("sp", hsm, hsm.ap, hsm_d[:, :])
    load("sp", hsv, hsv.ap, hsvd_d[:, :])
    ts("dve", hsv05.ap, hsv.ap, 0.05, None, ALU.mult, None, [hsv], [hsv05])
    dcv = dftc_d.rearrange("(kt p) n -> p kt n", p=128)
    dsv = dfts_d.rearrange("(kt p) n -> p kt n", p=128)
    for hh in range(2):
        load("pool", Cm, Cm.ap[:, :, hh * 512:(hh + 1) * 512], dcv[:, :, hh * 512:(hh + 1) * 512])
        load("pool", Sm, Sm.ap[:, :, hh * 512:(hh + 1) * 512], dsv[:, :, hh * 512:(hh + 1) * 512])
    tt("dve", identf.ap, MLE, MGE, ALU.mult, [ctab], [identf])
    cp("dve", identb.ap, identf.ap, [identf], [identb])
    memset("dve", onesb.ap, 1.0, [onesb])
    memset("dve", onesf.ap, 1.0, [onesf])
    act(cs.ap, cfm.ap, AF.Silu, [cfm], [cs])
    BM = vecs.ap[:, 0:1]
    TWON = vecs.ap[:, 1:2]

    def RMF(c):
        return vecs.ap[:, 2 + c:3 + c]

    def RMB(c):
        return vecs.ap[:, 10 + c:11 + c]

    def rstd_of(src):
        rstd = arena.alloc([128, T], F32)
        sqs = [arena.alloc([128, 512], BF16) for _ in range(3)]
        for half in range(2):
            hs = slice(half * 512, (half + 1) * 512)
            ps = bank()
            for kt in range(8):
                sq = sqs[kt % 3]
                if kt % 2 == 0:
                    act(sq.ap, src[kt].ap[:, hs], AF.Square, [src[kt]], [sq])
                else:
                    tt("dve", sq.ap, src[kt].ap[:, hs], src[kt].ap[:, hs], ALU.mult, [src[kt]], [sq])
                mm(ps.ap, onesb.ap, sq.ap, kt == 0, kt == 7, [onesb, sq], [ps])
            act(rstd.ap[:, hs], ps.ap, AF.Sqrt, [ps], [rstd], scale=1.0 / D, bias=EPS)
            P.op("dve", lambda e, hs=hs: e.reciprocal(out=rstd.ap[:, hs], in_=rstd.ap[:, hs]), reads=[rstd],
                 writes=[rstd])
        return rstd

    def pre_norm(gcol, shcol):
        m0 = arena.mark()
        rstd = rstd_of(x)
        tmps = [arena.alloc([128, T], F32) for _ in range(2)]
        for kt in range(8):
            tmp = tmps[kt % 2]
            stt(tmp.ap, x[kt].ap, gcol[0](kt), rstd.ap, ALU.mult, ALU.mult, [x[kt], gcol[1], rstd], [tmp])
            act(hb[kt].ap, tmp.ap, AF.Identity, [tmp, shcol[1]], [hb[kt]], bias=shcol[0](kt), scale=1.0)
        arena.reset(m0)

    def post_norm_residual(yo, gcol):
        m0 = arena.mark()
        rstd = rstd_of(yo)
        tmps = [arena.alloc([128, T], F32) for _ in range(2)]
        for kt in range(8):
            tmp = tmps[kt % 2]
            stt(tmp.ap, yo[kt].ap, gcol[0](kt), rstd.ap, ALU.mult, ALU.mult, [yo[kt], gcol[1], rstd], [tmp])
            tt("pool" if kt % 2 else "dve", x[kt].ap, x[kt].ap, tmp.ap, ALU.add, [x[kt], tmp], [x[kt]])
        arena.reset(m0)

    def conv3(src, src_v, cv, col, tapbase, bcol, fixbase, fixcol):
        w0 = lp.ap[:, tapbase[0] + col: tapbase[0] + col + 1]
        w1 = lp.ap[:, tapbase[1] + col: tapbase[1] + col + 1]
        w2 = lp.ap[:, tapbase[2] + col: tapbase[2] + col + 1]
        bb = lp.ap[:, bcol: bcol + 1]
        n0 = cvx.ap[:, fixbase + fixcol: fixbase + fixcol + 1]
        n2 = cvx.ap[:, 38 + fixbase + fixcol: 38 + fixbase + fixcol + 1]
        act(cv.ap, src, AF.Identity, [src_v, lp], [cv], scale=w1, bias=bb)
        stt(cv.ap[:, 1:T], src[:, 0:T - 1], w0, cv.ap[:, 1:T], ALU.mult, ALU.add, [src_v, lp, cv], [cv])
        stt(cv.ap[:, 0:T - 1], src[:, 1:T], w2, cv.ap[:, 0:T - 1], ALU.mult, ALU.add, [src_v, lp, cv], [cv])
        stt(cv.ap[:, 256:T:256], src[:, 255:T - 1:256], n0, cv.ap[:, 256:T:256], ALU.mult, ALU.add,
            [src_v, cvx, cv], [cv])
        stt(cv.ap[:, 255:T - 1:256], src[:, 256:T:256], n2, cv.ap[:, 255:T - 1:256], ALU.mult, ALU.add,
            [src_v, cvx, cv], [cv])

    def head_post(pso_ap, pso_v, gcols, c, k0, sb_src=False):
        m0 = arena.mark()
        sq = arena.alloc([128, 4, DH], F32)
        ss = arena.alloc([128, 4], F32)
        on = sq
        mx = arena.alloc([128, 384], BF16)
        act(sq.ap, pso_ap, AF.Square, [pso_v], [sq])
        P.op("dve", lambda e: e.tensor_reduce(out=ss.ap, in_=sq.ap, axis=AX.X, op=ALU.add), reads=[sq], writes=[ss])
        act(ss.ap, ss.ap, AF.Sqrt, [ss], [ss], scale=1.0 / DH, bias=EPS)
        P.op("dve", lambda e: e.reciprocal(out=ss.ap, in_=ss.ap), reads=[ss], writes=[ss])
        tt("dve", mx.ap.rearrange("p (a b) -> p a b", a=4), pso_ap, ss.ap.unsqueeze(2).broadcast_to([128, 4, DH]),
           ALU.mult, [pso_v, ss], [mx])
        pt = bank()
        ptb = pt.ap.bitcast(BF16)
        for j in range(3):
            tr(ptb[:, j * 128:(j + 1) * 128], mx.ap[:, j * 128:(j + 1) * 128], identb.ap, [mx, identb], [pt])
        cp("act", mixT.ap[:, k0:k0 + 3, c * 128:(c + 1) * 128],
           ptb[:, 0:384].rearrange("p (a b) -> p a b", a=3), [pt], [mixr(k0, k0 + 3)])
        arena.reset(m0)

    def tm_proj(l, col0, ncols, evac):
        wv = wintm_d[l].rearrange("(kt p) n -> p kt n", p=128)
        half_ = ncols // 2
        pans = [(0, half_), (half_, ncols - half_)]
        ss = [ring_load(wv[:, :, col0 + o_: col0 + o_ + n_], 8, n_) for (o_, n_) in pans]
        for c in range(NCH):
            ps = bank()
            for (o_, n_), s in zip(pans, ss):
                for kt in range(8):
                    mm(ps.ap[:, o_:o_ + n_], hb[kt].ap[:, c * 128:(c + 1) * 128], s.ap[:, kt, 0:n_], kt == 0, kt == 7,
                       [hb[kt], s], [ps])
            evac(c, ps)

    def fm_proj(wview, ntiles, evac):
        s = None
        for f in range(ntiles):
            if f % 2 == 0:
                n_ = min(256, (ntiles - f) * 128)
                s = ring_load(wview[:, :, f * 128: f * 128 + n_], 8, n_)
            j = f % 2
            for half in range(2):
                ps = bank()
                for kt in range(8):
                    mm(ps.ap, s.ap[:, kt, j * 128:(j + 1) * 128], hb[kt].ap[:, half * 512:(half + 1) * 512],
                       kt == 0, kt == 7, [s, hb[kt]], [ps])
                evac(f, half, ps)

    def transpose_heads(src_tm, c, dsts):
        pt = bank()
        ptb = pt.ap.bitcast(BF16)
        for i, s in enumerate(src_tm):
            for h in range(4):
                tr(ptb[0:DH, i * 512 + h * 128: i * 512 + (h + 1) * 128], s.ap[:, c, h * DH:(h + 1) * DH], identb.ap,
                   [s, identb], [pt])
        for i, d in enumerate(dsts):
            cp("act", d.ap,
               ptb[0:DH, i * 512:(i + 1) * 512].rearrange("p (a b) -> p a b", a=4), [pt], [d])

    mod_gen = [iter(())]

    def mod_panels(l):
        wmv = wmod_d[l].rearrange("(kt p) n -> p kt n", p=128)
        dst = modraw[l % 2]
        for p_ in range(24):
            s = ring_load(wmv[:, :, p_ * 256:(p_ + 1) * 256], 8, 256)
            ps = bank()
            for j in range(2):
                for kt in range(8):
                    mm(ps.ap[:, j:j + 1], s.ap[:, kt, j * 128:(j + 1) * 128], cs.ap[:, kt:kt + 1], kt == 0, kt == 7,
                       [s, cs], [ps])
            cp("act", dst.ap[:, p_ * 2:p_ * 2 + 2], ps.ap[:, 0:2], [ps], [dst])
            yield p_

    def mod_step():
        next(mod_gen[0], None)

    def layer(l):
        arena.reset(0)
        load("sp", lp, lp.ap, lp_d[l])
        load("sp", lrep, lrep.ap, lrep_d[l])
        load("sp", hmlp, hmlp.ap, hmlp_d[l])
        load("sp", sret, sret.ap, sret_d[l])
        load("sp", scn, scn.ap, scn_d[l])
        load("sp", smm, smm.ap, smm_d[l])

        if l == 0:
            for _ in mod_panels(0):
                pass
        else:
            for _ in mod_gen[0]:
                pass
        mod_gen[0] = mod_panels(l + 1) if l + 1 < nlayers else iter(())
        tt("dve", modt.ap, modraw[l % 2].ap, lp.ap[:, 32:80], ALU.add, [modraw[l % 2], lp], [modt])
        stt(gmt.ap[:, 0:8], modt.ap[:, 8:16], 1.0, lp.ap[:, 0:8], ALU.add, ALU.mult, [modt, lp], [gmt])
        tt("dve", gmt.ap[:, 8:16], modt.ap[:, 16:24], lp.ap[:, 8:16], ALU.mult, [modt, lp], [gmt])
        stt(gmt.ap[:, 16:24], modt.ap[:, 32:40], 1.0, lp.ap[:, 16:24], ALU.add, ALU.mult, [modt, lp], [gmt])
        tt("dve", gmt.ap[:, 24:32], modt.ap[:, 40:48], lp.ap[:, 24:32], ALU.mult, [modt, lp], [gmt])
        ts("dve", cvx.ap[:, 0:32], lp.ap[:, 80:112], BM, -1.0, ALU.mult, ALU.mult, [lp, vecs], [cvx])
        ts("dve", cvx.ap[:, 32:38], lp.ap[:, 208:214], BM, -1.0, ALU.mult, ALU.mult, [lp, vecs], [cvx])
        ts("dve", cvx.ap[:, 38:70], lp.ap[:, 144:176], BM, -1.0, ALU.mult, ALU.mult, [lp, vecs], [cvx])
        ts("dve", cvx.ap[:, 70:76], lp.ap[:, 220:226], BM, -1.0, ALU.mult, ALU.mult, [lp, vecs], [cvx])

        DL = lrep.ap[:, 784:792]
        act(lgt.ap, DL, AF.Exp, [lrep], [lgt], scale=-1.0)
        act(lgt.ap, lgt.ap, AF.Ln, [lgt], [lgt], bias=1.0, scale=1.0)
        P.op("act", lambda e: e.mul(out=lgt.ap, in_=lgt.ap, mul=-1.0), reads=[lgt], writes=[lgt], osz=8)
        m0 = arena.mark()
        ta = arena.alloc([128, 128], F32)
        tb = arena.alloc([128, 128], F32)
        for h in range(4):
            act(ta.ap, RP, AF.Exp, [ctab, lgt], [ta], scale=lgt.ap[:, h:h + 1])
            tt("pool", ta.ap, ta.ap, MLE, ALU.mult, [ta, ctab], [ta])
            act(tb.ap, RN, AF.Exp, [ctab, lgt], [tb], scale=lgt.ap[:, 4 + h:5 + h])
            tt("pool", tb.ap, tb.ap, MGE, ALU.mult, [tb, ctab], [tb])
            tt("dve", ta.ap, ta.ap, tb.ap, ALU.add, [ta, tb], [ta])
            P.op("act", lambda e, h=h: e.mul(out=dmask.ap[:, h, :], in_=ta.ap, mul=SCL), reads=[ta], writes=[dmask],
                 osz=128)
            act(qdec.ap[:, 0, h, :], IDX1[0:DH, :], AF.Exp, [ctab, lgt], [qdec], scale=lgt.ap[0:DH, h:h + 1])
            act(qdec.ap[:, 1, h, :], IDXR[0:DH, :], AF.Exp, [ctab, lgt], [qdec], scale=lgt.ap[0:DH, 4 + h:5 + h])
        tk = arena.alloc([128, 8], F32)
        tt("dve", tk.ap[:, 0:4], JREV4, lgt.ap[:, 0:4], ALU.mult, [ctab, lgt], [tk])
        tt("dve", tk.ap[:, 4:8], J4, lgt.ap[:, 4:8], ALU.mult, [ctab, lgt], [tk])
        act(kdec.ap, tk.ap, AF.Exp, [tk], [kdec])
        P.op("act", lambda e: e.mul(out=kdec.ap, in_=kdec.ap, mul=SCL), reads=[kdec], writes=[kdec], osz=8)
        act(cdec.ap, lgt.ap[0:DH, :], AF.Exp, [lgt], [cdec], scale=128.0)
        arena.reset(m0)
        FR, B1, B2 = hmlp.ap[0:64, 1152:1153], hmlp.ap[0:64, 1153:1154], hmlp.ap[0:64, 1154:1155]
        ts("dve", frs.ap[:, 0:1], FR, 1.0 / 3.0, None, ALU.mult, None, [hmlp], [frs])
        tt("dve", frs.ap[:, 1:2], frs.ap[:, 0:1], B1, ALU.mult, [frs, hmlp], [frs])
        tt("dve", frs.ap[:, 2:3], frs.ap[:, 0:1], B2, ALU.mult, [frs, hmlp], [frs])
        cp("dve", w3b.ap, hmlp.ap[:, 128:1152], [hmlp], [w3b])

        if stop == 'mod':
            return
        pre_norm((lambda kt: gmt.ap[:, kt:kt + 1], gmt), (lambda kt: modt.ap[:, kt:kt + 1], modt))
        dump("h1", hb[0], hb[0].ap, [128, T])

        if stop == 'norm1':
            return
        am = arena.mark()
        q_tm = arena.alloc([128, NCH, 384], BF16)
        k_tm = arena.alloc([128, NCH, 384], BF16)
        v_tm = arena.alloc([128, NCH, 384], BF16)
        gts = arena.alloc([128, NCH, 16], F32)
        rt = [arena.alloc([128, 4, DH], F32) for _ in range(4)]

        def rope_evac(dst):
            def f(c, ps):
                xv = ps.ap[:, 0:384].rearrange("p (a b) -> p a b", a=4)
                cos2 = rope.ap[:, 0, c, :].unsqueeze(1).broadcast_to([128, 4, DH])
                t1, t2 = rt[(c % 2) * 2], rt[(c % 2) * 2 + 1]
                tt("dve", t1.ap, xv, cos2, ALU.mult, [ps, rope], [t1])
                xsw = ps.ap[:, 0:384].rearrange("p (a h b) -> p a h b", a=4, h=2)[:, :, ::-1, :]
                sn4 = rope.ap[:, 1, c, :].rearrange("p (h b) -> p h b", h=2).unsqueeze(1).broadcast_to([128, 4, 2, 48])
                tt("dve", t2.ap.rearrange("p a (h b) -> p a h b", h=2), xsw, sn4, ALU.mult, [ps, rope], [t2])
                tt("pool", dst.ap[:, c, :].rearrange("p (a b) -> p a b", a=4), t1.ap, t2.ap, ALU.add, [t1, t2], [dst])
            return f

        def q_evac(c, ps):
            rope_evac(q_tm)(c, ps)
            tt("dve", gates_p.ap[:, c, :], ps.ap[:, 384:400], lrep.ap[:, 768:784], ALU.add, [ps, lrep], [gates_p])

        tm_proj(l, 0, 400, q_evac)
        tm_proj(l, 400, 384, rope_evac(k_tm))
        tm_proj(l, 784, 384, lambda c, ps: cp("act", v_tm.ap[:, c, :], ps.ap[:, 0:384], [ps], [v_tm]))
        dump("q_tm", q_tm, q_tm.ap[:, 0, :], [128, 384])
        if stop == 'ret_proj':
            return

        Sb_bf = arena.alloc([DH, NCH, 4, DH], BF16)
        Sf_bf = [arena.alloc([DH, 4, DH], BF16) for _ in range(2)]
        Scur = arena.alloc([DH, 4, DH], F32)
        Stmp = arena.alloc([DH, 4, DH], F32)
        Sout = [arena.alloc([DH, 4, DH], F32) for _ in range(2)]
        kd = arena.alloc([128, 4, DH], BF16)
        qTc = arena.alloc([DH, 4, 128], BF16)
        kTc = arena.alloc([DH, 4, 128], BF16)
        qdf = arena.alloc([DH, 4, 128], BF16)
        qdb = arena.alloc([DH, 4, 128], BF16)
        Pc = arena.alloc([128, 4, 128], BF16)
        so_i = [0]

        def ret_scan_step(dirn, c, psA):
            cd = cdec.ap[:, dirn * 4:(dirn + 1) * 4].unsqueeze(2).broadcast_to([DH, 4, DH])
            tt("dve", Stmp.ap, Scur.ap, cd, ALU.mult, [Scur, cdec], [Stmp])
            so = Sout[so_i[0] % 2]
            so_i[0] += 1
            tt("dve", so.ap, Stmp.ap, psA.ap[0:DH, 0:384].rearrange("p (a b) -> p a b", a=4), ALU.add, [Stmp, psA], [so])
            is_end = (c % 2 == 1) if dirn == 0 else (c % 2 == 0)
            if is_end:
                store(oret_d[c // 2, l, dirn].rearrange("h d e -> d h e"), so, so.ap)
            return so

        cp("dve", Scur.ap, sret.ap[:, 1], [sret], [Scur])
        cp("act", Sb_bf.ap[:, NCH - 1], Scur.ap, [Scur], [Sb_bf])
        for c in range(NCH - 1, -1, -1):
            mod_step()
            tt("dve", kd.ap, k_tm.ap[:, c, :].rearrange("p (a b) -> p a b", a=4),
               kdec.ap[:, 4:8].unsqueeze(2).broadcast_to([128, 4, DH]), ALU.mult, [k_tm, kdec], [kd])
            psA = bank()
            for h in range(4):
                mm(psA.ap[0:DH, h * DH:(h + 1) * DH], kd.ap[:, h, :], v_tm.ap[:, c, h * DH:(h + 1) * DH], True, True,
                   [kd, v_tm], [psA])
            so = ret_scan_step(1, c, psA)
            if c > 0:
                act(Scur.ap, so.ap, AF.Identity, [so, vecs], [Scur], scale=RMB(c - 1)[0:DH, :])
                cp("act", Sb_bf.ap[:, c - 1], Scur.ap, [Scur], [Sb_bf])
        if stop == 'ret_passA':
            return
        ret_pending = [None]
        cp("dve", Scur.ap, sret.ap[:, 0], [sret], [Scur])
        cp("act", Sf_bf[0].ap, Scur.ap, [Scur], [Sf_bf[0]])
        for c in range(NCH):
            if stop == f'retB_c{c}':
                return
            b2 = c % 2
            mod_step()
            transpose_heads([q_tm, k_tm], c, [qTc, kTc])
            if stop == f'retB_tr{c}':
                return
            tt("dve", qdf.ap, qTc.ap, qdec.ap[:, 0], ALU.mult, [qTc, qdec], [qdf])
            tt("pool", qdb.ap, qTc.ap, qdec.ap[:, 1], ALU.mult, [qTc, qdec], [qdb])
            pss = bank()
            for h in range(4):
                mm(pss.ap[:, h * 128:(h + 1) * 128], kTc.ap[:, h, :], qTc.ap[:, h, :], True, True, [kTc, qTc], [pss])
            tt("dve", Pc.ap, pss.ap.rearrange("p (a b) -> p a b", a=4), dmask.ap, ALU.mult, [pss, dmask], [Pc])
            if stop == f'retB_pc{c}':
                return
            tt("dve", kd.ap, k_tm.ap[:, c, :].rearrange("p (a b) -> p a b", a=4),
               kdec.ap[:, 0:4].unsqueeze(2).broadcast_to([128, 4, DH]), ALU.mult, [k_tm, kdec], [kd])
            psA = bank()
            for h in range(4):
                mm(psA.ap[0:DH, h * DH:(h + 1) * DH], kd.ap[:, h, :], v_tm.ap[:, c, h * DH:(h + 1) * DH], True, True,
                   [kd, v_tm], [psA])
            pso = bank()
            hold(pso)
            sfb = Sf_bf[b2]
            for h in range(4):
                o_ = pso.ap[:, h * DH:(h + 1) * DH]
                mm(o_, Pc.ap[:, h, :], v_tm.ap[:, c, h * DH:(h + 1) * DH], True, False, [Pc, v_tm], [pso])
                mm(o_, qdf.ap[:, h, :], sfb.ap[:, h, :], False, False, [qdf, sfb], [pso])
                mm(o_, qdb.ap[:, h, :], Sb_bf.ap[:, c, h, :], False, True, [qdb, Sb_bf], [pso])
            if stop == f'retB_pso{c}':
                return
            so = ret_scan_step(0, c, psA)
            if stop == f'retB_scan{c}':
                return
            if c < NCH - 1:
                act(Scur.ap, so.ap, AF.Identity, [so, vecs], [Scur], scale=RMF(c + 1)[0:DH, :])
                cp("act", Sf_bf[1 - b2].ap, Scur.ap, [Scur], [Sf_bf[1 - b2]])
            if ret_pending[0] is not None:
                ret_pending[0]()
            ret_pending[0] = (lambda c=c, pso=pso: (release(pso),
                                                    head_post(pso.ap[:, 0:384].rearrange("p (a b) -> p a b", a=4),
                                                              pso, 0, c, 0)))
        ret_pending[0]()
        arena.reset(am)

        if stop == 'ret':
            return
        am = arena.mark()
        mq_tm = arena.alloc([128, NCH, 384], BF16)
        mk_tm = arena.alloc([128, NCH, 384], BF16)
        mv1 = arena.alloc([128, NCH, 4, DH + 1], BF16)
        memset("pool", mv1.ap[:, :, :, DH:DH + 1], 1.0, [mv1])
        tm_proj(l, 1552, 384, lambda c, ps: cp("act", mq_tm.ap[:, c, :], ps.ap[:, 0:384], [ps], [mq_tm]))
        tm_proj(l, 1936, 384, lambda c, ps: P.op(
            "act", lambda e, c=c, ps=ps: e.mul(out=mk_tm.ap[:, c, :], in_=ps.ap[:, 0:384], mul=SCL), reads=[ps],
            writes=[mk_tm]))
        tm_proj(l, 2320, 384, lambda c, ps: cp("act", mv1.ap[:, c, :, 0:DH],
                                               ps.ap[:, 0:384].rearrange("p (a b) -> p a b", a=4), [ps], [mv1]))
        g5 = gates_p.ap.rearrange("p c (d k h) -> p c d k h", d=2, k=2)
        ipre = g5[:, :, :, 0, :]
        fpre = g5[:, :, :, 1, :]
        lf = arena.alloc([128, NCH, 2, 4], F32)
        wg = arena.alloc([128, NCH, 2, 4], F32)
        ew = arena.alloc([128, NCH, 2, 4], F32)
        eb = arena.alloc([128, NCH, 2, 4], F32)
        ebt = arena.alloc([DH, NCH, 8], F32)
        btR = arena.alloc([DH, NCH, 8], F32)
        wmR = arena.alloc([DH, NCH, 8], F32)
        act(lf.ap, fpre, AF.Exp, [gates_p], [lf], scale=-1.0)
        act(lf.ap, lf.ap, AF.Ln, [lf], [lf], bias=1.0, scale=1.0)
        ts("dve", lf.ap, lf.ap, -1.0, None, ALU.mult, None, [lf], [lf])
        psb = bank()
        psbt = bank()
        for c in range(NCH):
            mm(psb.ap[:, c * 8:c * 8 + 4], MLE, lf.ap[:, c, 0, :], True, True, [ctab, lf], [psb])
            mm(psb.ap[:, c * 8 + 4:c * 8 + 8], MGE, lf.ap[:, c, 1, :], True, True, [ctab, lf], [psb])
            mm(psbt.ap[0:DH, c * 8:(c + 1) * 8], onesf.ap[:, 0:DH], lf.ap[:, c].rearrange("p a b -> p (a b)"), True,
               True, [onesf, lf], [psbt])
        bps = psb.ap[:, 0:64].rearrange("p (c d h) -> p c d h", c=NCH, d=2)
        cp("dve", eb.ap, bps, [psb], [eb])
        tt("dve", wg.ap, ipre, eb.ap, ALU.subtract, [gates_p, eb], [wg])
        act(ew.ap, wg.ap, AF.Exp, [wg], [ew])
        act(eb.ap, eb.ap, AF.Exp, [eb], [eb], scale=-1.0)
        cp("dve", btR.ap, psbt.ap[0:DH, 0:64].rearrange("p (c n) -> p c n", c=NCH), [psbt], [btR])
        act(ebt.ap, btR.ap, AF.Exp, [btR], [ebt])
        pw = bank()
        tr(pw.ap[0:64, 0:128], wg.ap.rearrange("p c d h -> p (c d h)"), identf.ap, [wg, identf], [pw])
        wmc = arena.alloc([64, 1], F32)
        P.op("dve", lambda e: e.tensor_reduce(out=wmc.ap, in_=pw.ap[0:64, 0:128], axis=AX.X, op=ALU.max), reads=[pw],
             writes=[wmc])
        dg = arena.alloc([64, 64], F32)
        ts("dve", dg.ap, identf.ap[0:64, 0:64], wmc.ap[:, 0:1], None, ALU.mult, None, [identf, wmc], [dg])
        pw2 = bank()
        mm(pw2.ap[0:DH, 0:64], onesf.ap[0:64, 0:DH], dg.ap, True, True, [onesf, dg], [pw2])
        cp("dve", wmR.ap, pw2.ap[0:DH, 0:64].rearrange("p (c n) -> p c n", c=NCH), [pw2], [wmR])

        Gb_bf = arena.alloc([DH, NCH, 4, DH + 1], BF16)
        Gf_bf = [arena.alloc([DH, 4, DH + 1], BF16) for _ in range(2)]
        Gcur = arena.alloc([DH, 4, DH + 1], F32)
        Gtmp = arena.alloc([DH, 4, DH + 1], F32)
        Gout = arena.alloc([DH, 4, DH + 1], F32)
        mcur = arena.alloc([DH, 4], F32)
        mtmp = arena.alloc([DH, 4], F32)
        mout = [arena.alloc([DH, 4], F32) for _ in range(2)]
        emo = arena.alloc([DH, 4], F32)
        em0 = arena.alloc([DH, 8], F32)
        V2 = arena.alloc([128, 2, 4, DH + 1], BF16)
        Vf = V(V2.ap[:, 0], V2.regs)
        Vb = V(V2.ap[:, 1], V2.regs)
        mqTc = arena.alloc([DH, 4, 128], BF16)
        mkTc = arena.alloc([DH, 4, 128], BF16)
        PT2 = arena.alloc([128, 2, 4, 128], BF16)
        PTf = V(PT2.ap[:, 0], PT2.regs)
        PTb = V(PT2.ap[:, 1], PT2.regs)
        dn = arena.alloc([128, 2, 4], F32)
        hd = [arena.alloc([128, 4, DH], F32) for _ in range(2)]
        go_i = [0]
        act(em0.ap, smm.ap[0:DH, :], AF.Exp, [smm], [em0])

        def ml_scan_step(dirn, c, psU):
            tt("dve", Gtmp.ap, Gcur.ap, psU.ap[0:DH, 0:388].rearrange("p (a b) -> p a b", a=4), ALU.add, [Gcur, psU],
               [Gtmp])
            go, gs = Gout, Gtmp
            mo = mout[go_i[0] % 2]
            go_i[0] += 1
            tt("dve", go.ap, Gtmp.ap, ebt.ap[:, c, dirn * 4:(dirn + 1) * 4].unsqueeze(2).broadcast_to([DH, 4, DH + 1]),
               ALU.mult, [Gtmp, ebt], [go])
            tt("dve", mtmp.ap, mcur.ap, wmR.ap[:, c, dirn * 4:(dirn + 1) * 4], ALU.max, [mcur, wmR], [mtmp])
            tt("pool", mo.ap, mtmp.ap, btR.ap[:, c, dirn * 4:(dirn + 1) * 4], ALU.add, [mtmp, btR], [mo])
            is_end = (c % 2 == 1) if dirn == 0 else (c % 2 == 0)
            if is_end:
                act(emo.ap, mo.ap, AF.Exp, [mo], [emo], scale=-1.0)
                tt("dve", gs.ap, go.ap, emo.ap.unsqueeze(2).broadcast_to([DH, 4, DH + 1]), ALU.mult, [go, emo], [gs])
                store(oc_d[c // 2, l, dirn].rearrange("h d e -> d h e"), gs, gs.ap[:, :, 0:DH])
                store(on_d[c // 2, l, dirn].rearrange("h d -> d h"), gs, gs.ap[:, :, DH], slow=True)
                store(om_d[c // 2, l, dirn:dirn + 1, :], mo, mo.ap[0:1, :])
            return go, mo

        tt("dve", Gcur.ap, scn.ap[:, 1], em0.ap[:, 4:8].unsqueeze(2).broadcast_to([DH, 4, DH + 1]), ALU.mult,
           [scn, em0], [Gcur])
        cp("act", Gb_bf.ap[:, NCH - 1], Gcur.ap, [Gcur], [Gb_bf])
        cp("dve", mcur.ap, smm.ap[0:DH, 4:8], [smm], [mcur])
        for c in range(NCH - 1, -1, -1):
            mod_step()
            tt("dve", Vb.ap, mv1.ap[:, c], ew.ap[:, c, 1, :].unsqueeze(2).broadcast_to([128, 4, DH + 1]), ALU.mult,
               [mv1, ew], [Vb])
            psU = bank()
            for h in range(4):
                mm(psU.ap[0:DH, h * 97:(h + 1) * 97], mk_tm.ap[:, c, h * DH:(h + 1) * DH], Vb.ap[:, h, :], True, True,
                   [mk_tm, Vb], [psU])
            go, mo = ml_scan_step(1, c, psU)
            if c > 0:
                act(Gcur.ap, go.ap, AF.Identity, [go, vecs], [Gcur], scale=RMB(c - 1)[0:DH, :])
                cp("act", Gb_bf.ap[:, c - 1], Gcur.ap, [Gcur], [Gb_bf])
                act(mcur.ap, mo.ap, AF.Identity, [mo, vecs], [mcur], scale=RMB(c - 1)[0:DH, :])
        ml_pending = [None]
        tt("dve", Gcur.ap, scn.ap[:, 0], em0.ap[:, 0:4].unsqueeze(2).broadcast_to([DH, 4, DH + 1]), ALU.mult,
           [scn, em0], [Gcur])
        cp("act", Gf_bf[0].ap, Gcur.ap, [Gcur], [Gf_bf[0]])
        cp("dve", mcur.ap, smm.ap[0:DH, 0:4], [smm], [mcur])
        for c in range(NCH):
            b2 = c % 2
            mod_step()
            transpose_heads([mq_tm, mk_tm], c, [mqTc, mkTc])
            pss = bank()
            for h in range(4):
                mm(pss.ap[:, h * 128:(h + 1) * 128], mkTc.ap[:, h, :], mqTc.ap[:, h, :], True, True, [mkTc, mqTc], [pss])
            s4 = pss.ap.rearrange("p (a b) -> p a b", a=4)
            msk2 = ctab.ap[:, 256:512].rearrange("p (d i) -> p d i", d=2).unsqueeze(2).broadcast_to([128, 2, 4, 128])
            tt("dve", PT2.ap, s4.unsqueeze(1).broadcast_to([128, 2, 4, 128]), msk2, ALU.mult, [pss, ctab], [PT2])
            tt("dve", V2.ap, mv1.ap[:, c].unsqueeze(1).broadcast_to([128, 2, 4, DH + 1]),
               ew.ap[:, c].unsqueeze(3).broadcast_to([128, 2, 4, DH + 1]), ALU.mult, [mv1, ew], [V2])
            psU = bank()
            for h in range(4):
                mm(psU.ap[0:DH, h * 97:(h + 1) * 97], mk_tm.ap[:, c, h * DH:(h + 1) * DH], Vf.ap[:, h, :], True, True,
                   [mk_tm, Vf], [psU])
            psN = [bank(), bank()]
            hold(psN[0])
            hold(psN[1])
            gfb = Gf_bf[b2]
            for h in range(4):
                mm(psN[0].ap[:, h * 97:(h + 1) * 97], PTf.ap[:, h, :], Vf.ap[:, h, :], True, False, [PTf, Vf], [psN[0]])
                mm(psN[0].ap[:, h * 97:(h + 1) * 97], mqTc.ap[:, h, :], gfb.ap[:, h, :], False, True, [mqTc, gfb],
                   [psN[0]])
            for h in range(4):
                mm(psN[1].ap[:, h * 97:(h + 1) * 97], PTb.ap[:, h, :], Vb.ap[:, h, :], True, False, [PTb, Vb], [psN[1]])
                mm(psN[1].ap[:, h * 97:(h + 1) * 97], mqTc.ap[:, h, :], Gb_bf.ap[:, c, h, :], False, True,
                   [mqTc, Gb_bf], [psN[1]])
            go, mo = ml_scan_step(0, c, psU)
            if c < NCH - 1:
                act(Gcur.ap, go.ap, AF.Identity, [go, vecs], [Gcur], scale=RMF(c + 1)[0:DH, :])
                cp("act", Gf_bf[1 - b2].ap, Gcur.ap, [Gcur], [Gf_bf[1 - b2]])
                act(mcur.ap, mo.ap, AF.Identity, [mo, vecs], [mcur], scale=RMF(c + 1)[0:DH, :])
            def ml_tail(c=c, psN=psN):
                release(psN[0])
                release(psN[1])
                n4s = [psN[dirn].ap[:, 0:388].rearrange("p (a b) -> p a b", a=4) for dirn in range(2)]
                for dirn in range(2):
                    act(dn.ap[:, dirn, :], n4s[dirn][:, :, DH], AF.Abs, [psN[dirn]], [dn])
                tt("dve", dn.ap, dn.ap, eb.ap[:, c], ALU.max, [dn, eb], [dn])
                P.op("dve", lambda e: e.reciprocal(out=dn.ap, in_=dn.ap), reads=[dn], writes=[dn])
                for dirn in range(2):
                    tt("dve", hd[dirn].ap, n4s[dirn][:, :, 0:DH],
                       dn.ap[:, dirn, :].unsqueeze(2).broadcast_to([128, 4, DH]), ALU.mult, [psN[dirn], dn], [hd[dirn]])
                tt("dve", hd[0].ap, hd[0].ap, hd[1].ap, ALU.add, [hd[0], hd[1]], [hd[0]])
                head_post(hd[0].ap, hd[0], 384, c, 5, sb_src=True)

            if ml_pending[0] is not None:
                ml_pending[0]()
            ml_pending[0] = ml_tail
        ml_pending[0]()
        arena.reset(am)

        if stop == 'ml':
            return
        am = arena.mark()
        gtmp = [arena.alloc([128, 512], BF16) for _ in range(2)]
        wtv = wintm_d[l].rearrange("(kt p) n -> p kt n", p=128)

        def gate_evac(func, k0, gcol0):
            def f(fi, half, ps):
                g_ = gtmp[(fi * 2 + half) % 2]
                hs_ = slice(half * 512, (half + 1) * 512)
                act(g_.ap, ps.ap, func, [ps], [g_])
                stt(mixT.ap[:, k0 + fi, hs_], g_.ap, lp.ap[:, gcol0 + fi:gcol0 + fi + 1], mixT.ap[:, k0 + fi, hs_],
                    ALU.mult, ALU.mult, [mixr(k0 + fi, k0 + fi + 1), g_, lp], [mixr(k0 + fi, k0 + fi + 1)])
            return f

        fm_proj(wtv[:, :, 1168:1552], 3, gate_evac(AF.Silu, 0, 232))
        fm_proj(wtv[:, :, 2704:3088], 3, gate_evac(AF.Sigmoid, 5, 235))
        arena.reset(am)
        dump("mix_ret", mixr(0, 1), mixT.ap[:, 0, :], [128, T])
        dump("mix_ml", mixr(5, 6), mixT.ap[:, 5, :], [128, T])

        if stop == 'gates':
            return
        am = arena.mark()
        hx2 = arena.alloc([128, 2, T], F32)
        hv_tm = arena.alloc([128, NCH, 256], BF16)
        hx1_tm = arena.alloc([128, NCH, 256], F32)
        m1 = arena.mark()
        hv = arena.alloc([128, 2, T], BF16)
        hx1 = arena.alloc([128, 2, T], F32)
        hrs = [arena.alloc([128, T], F32) for _ in range(2)]
        cvs_ = [arena.alloc([128, T], F32) for _ in range(2)]
        wfv = winfm_d[l].rearrange("(kt p) n -> p kt n", p=128)

        def hy_evac(f6, half, ps):
            hr, cv = hrs[f6 % 2], cvs_[f6 % 2]
            cp("act", hr.ap[:, half * 512:(half + 1) * 512], ps.ap, [ps], [hr])
            if half == 1:
                conv3(hr.ap, hr, cv, f6, (208, 214, 220), 226 + f6, 32, f6)
                dst = [hv, hv, hx1, hx1, hx2, hx2][f6]
                cp("pool", dst.ap[:, f6 % 2, :], cv.ap, [cv], [dst])

        fm_proj(wfv, 6, hy_evac)
        dump("hx2", hx2, hx2.ap[:, 0, :], [128, T])
        for c4 in range(2):
            pt = bank()
            ptb = pt.ap.bitcast(BF16)
            for cc in range(4):
                c = c4 * 4 + cc
                for j in range(2):
                    tr(ptb[:, cc * 256 + j * 128: cc * 256 + (j + 1) * 128], hv.ap[:, j, c * 128:(c + 1) * 128],
                       identb.ap, [hv, identb], [pt])
            cp("act", hv_tm.ap[:, c4 * 4:(c4 + 1) * 4, :], ptb.rearrange("p (a b) -> p a b", a=4), [pt], [hv_tm])
        for c2 in range(4):
            pt = bank()
            for cc in range(2):
                c = c2 * 2 + cc
                for j in range(2):
                    tr(pt.ap[:, cc * 256 + j * 128: cc * 256 + (j + 1) * 128], hx1.ap[:, j, c * 128:(c + 1) * 128],
                       identf.ap, [hx1, identf], [pt])
            cp("dve", hx1_tm.ap[:, c2 * 2:(c2 + 1) * 2, :], pt.ap.rearrange("p (a b) -> p a b", a=2), [pt], [hx1_tm])
        arena.reset(m1)
        h2aug = arena.alloc([65, 2, T], BF16)
        biasA = arena.alloc([128, 512], F32)
        hyb = arena.alloc([128, 512], F32)
        load("sp", hyb, hyb.ap, hyb_d[l])
        ts("dve", biasA.ap, hyb.ap, TWON, None, ALU.mult, None, [hyb, vecs], [biasA])
        m2 = arena.mark()
        hfeat = arena.alloc([33, 2, T], F32)
        load("sp", hfeat, hfeat.ap, hfeat_d[:, :, :])
        memset("dve", h2aug.ap[64:65, :, :], 1.0, [h2aug])
        hm1 = arena.alloc([64, T], F32)
        hs1 = arena.alloc([64, 512], F32)
        hs2 = arena.alloc([64, 512], F32)

        def sin3(ps, bias_col, out_ap, out_v):
            act(hs1.ap, ps.ap[0:64, :], AF.Sin, [ps, frs], [hs1], scale=frs.ap[:, 0:1], bias=bias_col)
            tt("dve", hs2.ap, hs1.ap, hs1.ap, ALU.mult, [hs1], [hs2])
            act(hs2.ap, hs2.ap, AF.Identity, [hs2], [hs2], scale=-4.0, bias=3.0)
            tt("dve", out_ap, hs1.ap, hs2.ap, ALU.mult, [hs1, hs2], [out_v])

        for d_ in range(2):
            for half in range(2):
                hs_ = slice(half * 512, (half + 1) * 512)
                ps = bank()
                mm(ps.ap[0:64, :], hmlp.ap[0:33, 0:64], hfeat.ap[:, d_, hs_], True, True, [hmlp, hfeat], [ps])
                sin3(ps, frs.ap[:, 1:2], hm1.ap[:, hs_], hm1)
            for half in range(2):
                hs_ = slice(half * 512, (half + 1) * 512)
                ps = bank()
                mm(ps.ap[0:64, :], hmlp.ap[0:64, 64:128], hm1.ap[:, hs_], True, True, [hmlp, hm1], [ps])
                sin3(ps, frs.ap[:, 2:3], h2aug.ap[0:64, d_, hs_], h2aug)
        arena.reset(m2)

        def AB(which, ft):
            k = which * 4 + ft // 2
            return V(hb[k].ap.bitcast(F32)[:, (ft % 2) * 256:(ft % 2 + 1) * 256], hb[k].regs)

        z_tm = arena.alloc([128, NCH, 256], BF16)
        m3 = arena.mark()
        for o in range(2):
            arena.reset(m3)
            fsum = arena.alloc([128, 8, 256], BF16)
            fdif = arena.alloc([128, 8, 256], BF16)
            ftm2 = arena.alloc([128, 2, 256], F32)
            wtm2 = [arena.alloc([128, 2, 256], F32) for _ in range(2)]
            e4 = [arena.alloc([128, 256], F32) for _ in range(2)]
            for tt_ in range(8):
                ps = bank()
                w2 = wtm2[tt_ % 2]
                for d_ in range(2):
                    mm(ps.ap[:, d_ * 256:(d_ + 1) * 256], h2aug.ap[0:65, d_, tt_ * 128:(tt_ + 1) * 128],
                       w3b.ap[0:65, (d_ * 2 + o) * 256:(d_ * 2 + o + 1) * 256], True, True, [h2aug, w3b], [ps])
                    act(w2.ap[:, d_, :], hnd.ap, AF.Exp, [hnd], [w2], scale=HTN(d_, tt_))
                    act(w2.ap[:, d_, :], w2.ap[:, d_, :], AF.Identity, [w2, hsv, hsv05], [w2],
                        scale=hsv.ap[:, d_ * 8 + tt_: d_ * 8 + tt_ + 1], bias=hsv05.ap[:, d_ * 8 + tt_: d_ * 8 + tt_ + 1])
                tt("dve", ftm2.ap, ps.ap.rearrange("p (d c) -> p d c", d=2), w2.ap, ALU.mult, [ps, w2], [ftm2])
                tt("pool", fsum.ap[:, tt_, :], ftm2.ap[:, 0, :], ftm2.ap[:, 1, :], ALU.add, [ftm2], [fsum])
                tt("dve", fdif.ap[:, tt_, :], ftm2.ap[:, 1, :], ftm2.ap[:, 0, :], ALU.subtract, [ftm2], [fdif])
            for ft in range(8):
                ps = bank()
                for tt_ in range(8):
                    mm(ps.ap[:, 0:256], Cm.ap[:, tt_, ft * 128:(ft + 1) * 128], fsum.ap[:, tt_, :], tt_ == 0, tt_ == 7,
                       [Cm, fsum], [ps])
                for tt_ in range(8):
                    mm(ps.ap[:, 256:512], Sm.ap[:, tt_, ft * 128:(ft + 1) * 128], fdif.ap[:, tt_, :], tt_ == 0,
                       tt_ == 7, [Sm, fdif], [ps])
                Kr, Ks = ps.ap[:, 0:256], ps.ap[:, 256:512]
                A_, B_ = AB(0, ft), AB(1, ft)
                stt(e4[0].ap, Ks, HNSIN[:, ft:ft + 1], biasA.ap[:, o * 256:(o + 1) * 256], ALU.mult, ALU.add,
                    [ps, hsm, biasA], [e4[0]])
                stt(A_.ap, Kr, HCOS[:, ft:ft + 1], e4[0].ap, ALU.mult, ALU.add, [ps, hsm, e4[0]], [A_])
                ts("dve", e4[1].ap, Ks, HCOS[:, ft:ft + 1], None, ALU.mult, None, [ps, hsm], [e4[1]])
                stt(B_.ap, Kr, HSIN[:, ft:ft + 1], e4[1].ap, ALU.mult, ALU.add, [ps, hsm, e4[1]], [B_])
            arena.reset(m3)
            Pq = arena.alloc([128, 8, 256], BF16)
            Qq = arena.alloc([128, 8, 256], BF16)
            e4 = [arena.alloc([128, 2, 256], F32) for _ in range(4)]
            zin = hv_tm if o == 0 else z_tm
            for m_ in range(4):
                ps2 = pair()
                for f2 in range(2):
                    ft = m_ * 2 + f2
                    base = f2 * 512
                    for tt_ in range(8):
                        mm(ps2.ap[:, base:base + 256], Cm.ap[:, tt_, ft * 128:(ft + 1) * 128], zin.ap[:, tt_, :],
                           tt_ == 0, tt_ == 7, [Cm, zin], [ps2.regs[f2]])
                    for tt_ in range(8):
                        mm(ps2.ap[:, base + 256:base + 512], Sm.ap[:, tt_, ft * 128:(ft + 1) * 128], zin.ap[:, tt_, :],
                           tt_ == 0, tt_ == 7, [Sm, zin], [ps2.regs[f2]])
                z4 = ps2.ap.rearrange("p (f x c) -> p f x c", f=2, x=2)
                Zc, Zs = z4[:, :, 0, :], z4[:, :, 1, :]
                A2 = V(hb[m_].ap.bitcast(F32).rearrange("p (f c) -> p f c", f=2), hb[m_].regs)
                B2 = V(hb[4 + m_].ap.bitcast(F32).rearrange("p (f c) -> p f c", f=2), hb[4 + m_].regs)
                tt("dve", e4[0].ap, Zc, A2.ap, ALU.mult, [ps2, A2], [e4[0]])
                tt("dve", e4[1].ap, Zs, B2.ap, ALU.mult, [ps2, B2], [e4[1]])
                tt("pool", Pq.ap[:, m_ * 2:m_ * 2 + 2, :], e4[0].ap, e4[1].ap, ALU.add, [e4[0], e4[1]], [Pq])
                tt("dve", e4[2].ap, Zs, A2.ap, ALU.mult, [ps2, A2], [e4[2]])
                tt("dve", e4[3].ap, Zc, B2.ap, ALU.mult, [ps2, B2], [e4[3]])
                tt("pool", Qq.ap[:, m_ * 2:m_ * 2 + 2, :], e4[2].ap, e4[3].ap, ALU.subtract, [e4[2], e4[3]], [Qq])
            if o == 0:
                for m_ in range(4):
                    ps = bank()
                    for f2 in range(2):
                        tt_ = m_ * 2 + f2
                        o_ = ps.ap[:, f2 * 256:(f2 + 1) * 256]
                        for ft in range(8):
                            mm(o_, Cm.ap[:, ft, tt_ * 128:(tt_ + 1) * 128], Pq.ap[:, ft, :], ft == 0, False,
                               [Cm, Pq], [ps])
                        for ft in range(8):
                            mm(o_, Sm.ap[:, ft, tt_ * 128:(tt_ + 1) * 128], Qq.ap[:, ft, :], False, ft == 7,
                               [Sm, Qq], [ps])
                    tt("dve", z_tm.ap[:, m_ * 2:m_ * 2 + 2, :], ps.ap.rearrange("p (f c) -> p f c", f=2),
                       hx1_tm.ap[:, m_ * 2:m_ * 2 + 2, :], ALU.mult, [ps, hx1_tm], [z_tm])
            else:
                for ct in range(2):
                    for half in range(2):
                        hs_ = slice(half * 512, (half + 1) * 512)
                        ps = bank()
                        for ft in range(8):
                            mm(ps.ap, Pq.ap[:, ft, ct * 128:(ct + 1) * 128], Cm.ap[:, ft, hs_], ft == 0, False,
                               [Cm, Pq], [ps])
                        for ft in range(8):
                            mm(ps.ap, Qq.ap[:, ft, ct * 128:(ct + 1) * 128], Sm.ap[:, ft, hs_], False, ft == 7,
                               [Sm, Qq], [ps])
                        tt("dve", mixT.ap[:, 3 + ct, hs_], ps.ap, hx2.ap[:, ct, hs_], ALU.mult, [ps, hx2],
                           [mixr(3 + ct, 4 + ct)])
        arena.reset(am)
        dump("mix_hy", mixr(3, 4), mixT.ap[:, 3, :], [128, T])

        if stop == 'hy':
            return
        am = arena.mark()
        yo = [arena.alloc([128, T], F32) for _ in range(8)]
        wov = wout_d[l].rearrange("(kt p) n -> p kt n", p=128)
        for pnl in range(4):
            s = ring_load(wov[:, :, pnl * 256:(pnl + 1) * 256], 8, 256)
            for j in range(2):
                dt_ = pnl * 2 + j
                for half in range(2):
                    hs_ = slice(half * 512, (half + 1) * 512)
                    ps = bank()
                    for kt in range(8):
                        mm(ps.ap, s.ap[:, kt, j * 128:(j + 1) * 128], mixT.ap[:, kt, hs_], kt == 0, kt == 7,
                           [s, mixr(kt, kt + 1)], [ps])
                    cp("act", yo[dt_].ap[:, hs_], ps.ap, [ps], [yo[dt_]])
        dump("mixout", yo[0], yo[0].ap, [128, T])
        post_norm_residual(yo, (lambda kt: gmt.ap[:, 8 + kt:9 + kt], gmt))
        arena.reset(am)
        dump("x1", x[0], x[0].ap, [128, T])

        if stop == 'wout':
            return
        pre_norm((lambda kt: gmt.ap[:, 16 + kt:17 + kt], gmt), (lambda kt: modt.ap[:, 24 + kt:25 + kt], modt))
        am = arena.mark()
        yo = [arena.alloc([128, T], F32) for _ in range(8)]
        m1 = arena.mark()
        gbuf = mixT
        cvs = [arena.alloc([128, T], F32) for _ in range(2)]
        gls = [arena.alloc([128, T], F32) for _ in range(2)]
        wuv = wup_d[l].rearrange("(kt p) n -> p kt n", p=128)
        wdv = wdown_d[l].rearrange("(ft p) n -> p ft n", p=128)
        for g in range(4):
            for j in range(4):
                sa = ring_load(wuv[:, :, g * 1024 + j * 256: g * 1024 + (j + 1) * 256], 8, 256)
                sb_ = ring_load(wuv[:, :, 4096 + g * 1024 + j * 256: 4096 + g * 1024 + (j + 1) * 256], 8, 256)
                for jj in range(2):
                    ft = j * 2 + jj
                    fidx = g * 8 + ft
                    psa = pair()
                    psb2 = pair()
                    for half in range(2):
                        hs_ = slice(half * 512, (half + 1) * 512)
                        for kt in range(8):
                            mm(psa.ap[:, hs_], sa.ap[:, kt, jj * 128:(jj + 1) * 128], hb[kt].ap[:, hs_], kt == 0,
                               kt == 7, [sa, hb[kt]], [psa.regs[half]])
                    for half in range(2):
                        hs_ = slice(half * 512, (half + 1) * 512)
                        for kt in range(8):
                            mm(psb2.ap[:, hs_], sb_.ap[:, kt, jj * 128:(jj + 1) * 128], hb[kt].ap[:, hs_], kt == 0,
                               kt == 7, [sb_, hb[kt]], [psb2.regs[half]])
                    cv = cvs[ft % 2]
                    gl = gls[ft % 2]
                    conv3(psa.ap, psa, cv, fidx, (80, 112, 144), 176 + fidx, 0, fidx)
                    act(gl.ap, cv.ap, AF.Gelu_apprx_tanh, [cv], [gl])
                    tt("dve", gbuf.ap[:, ft, :], gl.ap, psb2.ap, ALU.mult, [gl, psb2], [mixr(ft, ft + 1)])
            for pnl in range(4):
                s = ring_load(wdv[:, g * 8:(g + 1) * 8, pnl * 256:(pnl + 1) * 256], 8, 256)
                for j in range(2):
                    dt_ = pnl * 2 + j
                    for half in range(2):
                        hs_ = slice(half * 512, (half + 1) * 512)
                        ps = bank()
                        for ft in range(8):
                            mm(ps.ap, s.ap[:, ft, j * 128:(j + 1) * 128], gbuf.ap[:, ft, hs_], ft == 0, ft == 7,
                               [s, mixr(ft, ft + 1)], [ps])
                        if g == 0:
                            cp("act", yo[dt_].ap[:, hs_], ps.ap, [ps], [yo[dt_]])
                        else:
                            tt("dve", yo[dt_].ap[:, hs_], yo[dt_].ap[:, hs_], ps.ap, ALU.add, [yo[dt_], ps], [yo[dt_]])
        dump("ffn", yo[0], yo[0].ap, [128, T])
        arena.reset(m1)
        post_norm_residual(yo, (lambda kt: gmt.ap[:, 24 + kt:25 + kt], gmt))
        arena.reset(am)

    gates_p = P.sb("gates_p", [128, NCH, 16], F32)

    for l in range(nlayers):
        layer(l)

    fence = [P.sb(f"fence{i}", [1, 4], F32) for i in range(3)]
    psf = bank()
    mm(psf.ap[0:1, 0:1], onesb.ap[0:1, 0:1], onesb.ap[0:1, 0:1], True, True, [onesb], [psf])
    cp("act", fence[0].ap[:, 0:1], psf.ap[0:1, 0:1], [psf], [fence[0]])
    memset("dve", fence[1].ap, 0.0, [fence[1]])
    memset("pool", fence[2].ap, 0.0, [fence[2]])
    for k in range(8):
        P.op("sp", lambda e, k=k: e.dma_start(out=yT_d[k * 128:(k + 1) * 128, :], in_=x[k].ap),
             reads=[x[k]] + fence, dsem=_sem_for(st_sems, x[k], "st"))
    P.emit(final_dsems=list(st_sems.values()))
    build.info = dict(sb_bytes=P.sb_bytes, arena_peak=arena.peak, sig=P.sig_counts, nops=len(P.ops),
                      dbg_outs=dbg_outs)
    return nc


GRID_W = 64
ROPE_BASE = 10000.0
N_BANDS = 16
HY_W = 256


def _fm_cols(v):
    return np.ascontiguousarray(v.reshape(-1, 128).T)


def _const_tables(is_latent):
    f32 = np.float32
    L = 1024 if is_latent else 256
    S = T // L
    N = 2 * L
    t = np.arange(T)
    if is_latent:
        row = (t // GRID_W).astype(f32)
        col = (t % GRID_W).astype(f32)
        nf = (DH // 2) // 2
        freqs = (ROPE_BASE ** (-np.arange(nf, dtype=f32) / nf)).astype(f32)
        ang = np.concatenate([row[:, None] * freqs, col[:, None] * freqs], axis=-1).astype(f32)
        cos, sin = np.cos(ang), np.sin(ang)
    else:
        cos, sin = np.ones((T, 48), f32), np.zeros((T, 48), f32)
    cos2 = np.concatenate([cos, cos], -1)
    sin2 = np.concatenate([-sin, sin], -1)
    rope = np.stack([cos2, sin2], 0).reshape(2, NCH, 128, DH).transpose(2, 0, 1, 3).astype(f32)
    a = 2 * np.pi * np.outer(np.arange(L) + 0.5, np.arange(L) + 0.5) / N
    C1, S1 = np.cos(a), np.sin(a)
    Cm = np.zeros((T, T), f32)
    Smt = np.zeros((T, T), f32)
    for s in range(S):
        Cm[s * L:(s + 1) * L, s * L:(s + 1) * L] = C1
        Smt[s * L:(s + 1) * L, s * L:(s + 1) * L] = S1
    tau = np.arange(T) % L

    def feat(tv):
        tn = (tv / L).astype(f32)
        bands = np.linspace(1e-4, N_BANDS - 1, N_BANDS, dtype=f32)
        ang = (2.0 * math.pi * tn[:, None] * bands[None, :]).astype(f32)
        return np.concatenate([tn[:, None], np.cos(ang), np.sin(ang)], -1).astype(f32)

    hfeat = np.stack([feat(tau.astype(f32)).T, feat((tau + 1).astype(f32)).T], 1).astype(f32)
    th = np.pi * (tau + 0.5) / N
    deltas = np.abs(np.linspace(math.log(1e-2) / 1.5, math.log(1e-2) / 0.3, HY_W, dtype=f32))
    hsm = np.zeros((128, 40 + 256), f32)
    hsm[:, 0:8] = _fm_cols(np.cos(th).astype(f32))
    hsm[:, 8:16] = _fm_cols(np.sin(th).astype(f32))
    hsm[:, 16:24] = -hsm[:, 8:16]
    hsm[:, 24:32] = _fm_cols((tau / L).astype(f32))
    hsm[:, 32:40] = _fm_cols(((tau + 1) / L).astype(f32))
    hsm[:, 40:296] = -deltas[None, :]
    hsv = np.zeros((128, 16), f32)
    hsv[:, 0:8] = 2.0 / N
    hsv[:, 8:16] = _fm_cols((tau < L - 1).astype(f32) * (2.0 / N))
    return dict(rope=rope, dftC=Cm, dftS=Smt, hfeat=hfeat, hsm=hsm, hsvd=hsv), N


def _ctab():
    f32 = np.float32
    j = np.arange(128)[:, None].astype(f32)
    i = np.arange(128)[None, :].astype(f32)
    ct = np.zeros((128, NCT), f32)
    ct[:, 0:128] = np.maximum(i - j, 0)
    ct[:, 128:256] = np.maximum(j - i, 0)
    ct[:, 256:384] = (j <= i)
    ct[:, 384:512] = (j >= i)
    ct[:, 512:640] = np.broadcast_to(i + 1.0, (128, 128))
    ct[:, 640:768] = np.broadcast_to(128.0 - i, (128, 128))
    ct[:, 768:772] = np.broadcast_to(127.0 - j, (128, 4))
    ct[:, 772:776] = np.broadcast_to(j, (128, 4))
    return ct


def _prep(inputs):
    f32 = np.float32
    g = {k: np.asarray(v, dtype=f32) for k, v in inputs.items()}
    w_in = g["w_in"]
    rq, rk, rv, rg = [w_in[:, :, i * 384:(i + 1) * 384] for i in range(4)]
    hy = w_in[:, :, 1536:2304]
    mq, mk, mv, mo = [w_in[:, :, 2304 + i * 384: 2304 + (i + 1) * 384] for i in range(4)]
    gt = w_in[:, :, 3840:3856]
    w_in_tm = np.ascontiguousarray(np.concatenate([rq, gt, rk, rv, rg, mq, mk, mv, mo], axis=2))
    w_in_fm = np.ascontiguousarray(hy)
    lp = np.zeros((DEPTH, 128, NLP), f32)
    lrep = np.zeros((DEPTH, 128, NLR), f32)
    hmlp = np.zeros((DEPTH, 65, NHM), f32)
    hyb = np.zeros((DEPTH, 128, 512), f32)
    for l in range(DEPTH):
        lp[l, :, 0:8] = _fm_cols(g["norm_mix_pre"][l])
        lp[l, :, 8:16] = _fm_cols(g["norm_mix_post"][l])
        lp[l, :, 16:24] = _fm_cols(g["norm_ffn_pre"][l])
        lp[l, :, 24:32] = _fm_cols(g["norm_ffn_post"][l])
        lp[l, :, 32:80] = _fm_cols(g["b_mod"][l])
        for tap in range(3):
            lp[l, :, 80 + tap * 32: 112 + tap * 32] = _fm_cols(g["ffn_conv_w"][l, tap])
            lp[l, :, 208 + tap * 6: 214 + tap * 6] = _fm_cols(g["hy_conv_w"][l, tap])
        lp[l, :, 176:208] = _fm_cols(g["ffn_conv_b"][l])
        lp[l, :, 226:232] = _fm_cols(g["hy_conv_b"][l])
        lp[l, :, 232:235] = _fm_cols(g["ret_norm_g"][l])
        lp[l, :, 235:238] = _fm_cols(g["ml_norm_g"][l])
        lrep[l, :, 0:384] = g["ret_norm_g"][l][None]
        lrep[l, :, 384:768] = g["ml_norm_g"][l][None]
        hyb[l, :, :] = g["hy_bias"][l].reshape(-1)[None]
        lrep[l, :, 768:784] = g["ml_gate_bias"][l].reshape(-1)[None]
        lrep[l, :, 784:792] = g["ret_decay_logit"][l].reshape(-1)[None]
        hmlp[l, 0:33, 0:64] = g["hy_f_w1"][l]
        hmlp[l, 0:64, 64:128] = g["hy_f_w2"][l]
        hmlp[l, 0:64, 128:1152] = g["hy_f_w3"][l]
        hmlp[l, 64, 128:1152] = g["hy_f_b3"][l]
        hmlp[l, 0:64, 1152] = g["hy_sin_freq"][l]
        hmlp[l, 0:64, 1153] = g["hy_f_b1"][l]
        hmlp[l, 0:64, 1154] = g["hy_f_b2"][l]
    shared = dict(w_mod=g["w_mod"], w_in_tm=w_in_tm, w_in_fm=w_in_fm, w_out=g["w_out"], w_up=g["w_up"],
                  w_down=g["w_down"], lp=lp, lrep=lrep, hmlp=hmlp, hyb=hyb, ctab=_ctab())
    tabs = {False: _const_tables(False), True: _const_tables(True)}
    maps = []
    for core in range(8):
        is_lat = core >= 4
        ct, N = tabs[is_lat]
        m = dict(shared)
        m.update(ct)
        vecs = np.ones((128, NVEC), f32)
        if is_lat:
            b = (core - 4) % 2
            xT = g["x_sample"][b].T
            cvec = g["c"][b]
            vecs[:, 0] = 0.0
            sret = g["state_ret"][b].transpose(0, 3, 1, 2, 4)
            sc = g["state_mlstm_c"][b].transpose(0, 3, 1, 2, 4)
            sn = g["state_mlstm_n"][b].transpose(0, 3, 1, 2)[..., None]
            scn = np.concatenate([sc, sn], -1)
            smm = np.broadcast_to(g["state_mlstm_m"][b].reshape(DEPTH, 1, 8), (DEPTH, 128, 8))
        else:
            xT = g["x_prompt"][4 * core:4 * core + 4].reshape(T, D).T
            cvec = g["c_ctx"]
            vecs[:, 0] = 1.0
            for c in range(8):
                vecs[:, 2 + c] = 0.0 if c % 2 == 0 else 1.0
                vecs[:, 10 + c] = 0.0 if c % 2 == 1 else 1.0
            sret = np.zeros((DEPTH, DH, 2, 4, DH), f32)
            scn = np.zeros((DEPTH, DH, 2, 4, DH + 1), f32)
            smm = np.zeros((DEPTH, 128, 8), f32)
        vecs[:, 1] = 2.0 / N
        m.update(xT=xT, cfm=_fm_cols(cvec), vecs=vecs, sret=sret, scn=scn, smm=smm)
        maps.append({k: np.ascontiguousarray(v, dtype=f32) for k, v in m.items()})
    return maps


_CACHE = {}


def _assemble(r):
    f = np.float32
    yp = np.stack([r[k]["yT"].T.reshape(4, 256, D) for k in range(4)], 0).reshape(16, 256, D)
    ys = np.stack([r[4]["yT"].T, r[5]["yT"].T], 0)
    o_ret = np.concatenate([r[k]["o_ret"] for k in range(4)], 0)
    o_c = np.concatenate([r[k]["o_c"] for k in range(4)], 0)
    o_n = np.concatenate([r[k]["o_n"] for k in range(4)], 0)
    o_m = np.concatenate([r[k]["o_m"] for k in range(4)], 0)
    return (yp.astype(f), ys.astype(f), o_ret.astype(f), o_c.astype(f), o_n.astype(f), o_m.astype(f))


def kernel(**inputs):
    maps = _prep(inputs)
    if "nc" not in _CACHE:
        _CACHE["nc"] = build()
    res = run_bass_kernel_spmd(_CACHE["nc"], maps, core_ids=list(range(8)))
    return _assemble(res.results)
```

```python
from contextlib import ExitStack
import math
import numpy as np
import concourse.bass as bass
import concourse.mybir as mybir
from concourse.bass_utils import run_bass_kernel_spmd

F32 = mybir.dt.float32
BF16 = mybir.dt.bfloat16
AF = mybir.ActivationFunctionType
ALU = mybir.AluOpType
AX = mybir.AxisListType

D = 1024
T = 1024
DEPTH = 4
NCH = 8
DH = 96
EPS = 1e-6
SCL = DH ** -0.5
ENGS = ("pe", "act", "dve", "pool", "sp")
SAME_ENG_MIN = 1 << 30


class Region:
    __slots__ = ("last_w", "readers")

    def __init__(self):
        self.last_w = None
        self.readers = []


class V:
    def __init__(self, ap, regs):
        self.ap = ap
        self.regs = regs

    def __getitem__(self, idx):
        return self.ap[idx]


class DSem:
    def __init__(self, h):
        self.h = h
        self.count = 0


class Op:
    __slots__ = ("eng", "fn", "deps", "dsem", "dval", "sig", "need_sig", "idx", "line", "osz")

    def __init__(self, eng, fn, deps, dsem):
        self.eng = eng
        self.fn = fn
        self.deps = deps
        self.dsem = dsem
        self.dval = None
        self.sig = None
        self.need_sig = False


def _prod(s):
    n = 1
    for a in s:
        n *= a
    return n


class Prog:
    def __init__(self, nc):
        self.nc = nc
        self.es = ExitStack()
        self.ops = []
        self.sb_bytes = 0

    def sb(self, name, shape, dt, nreg=1):
        t = self.es.enter_context(self.nc.sbuf_tensor("s_" + name, list(shape), dt))
        self.sb_bytes += _prod(shape[1:]) * (4 if dt == F32 else 2)
        return V(t[:] if True else t, [Region() for _ in range(nreg)])

    def ps(self, name, shape, dt=F32, nreg=1):
        t = self.es.enter_context(self.nc.psum_tensor("p_" + name, list(shape), dt))
        return V(t[:], [Region() for _ in range(nreg)])

    def dsem(self, name):
        return DSem(self.es.enter_context(self.nc.semaphore(name)))

    def _regs(self, lst):
        out = []
        for x in lst:
            if x is None:
                continue
            if isinstance(x, Region):
                out.append(x)
            elif hasattr(x, "regs"):
                out.extend(x.regs)
            else:
                out.extend(self._regs(x))
        return out

    def op(self, eng, fn, reads=(), writes=(), dsem=None, osz=0):
        rr = self._regs(reads)
        ww = self._regs(writes)
        deps = set()
        for r in rr:
            if r.last_w is not None:
                deps.add(r.last_w)
        for w in ww:
            if w.last_w is not None:
                deps.add(w.last_w)
            deps.update(w.readers)
        o = Op(eng, fn, deps, dsem)
        o.osz = osz
        o.idx = len(self.ops)
        import sys as _s
        fr = _s._getframe(1)
        while fr.f_code.co_name in ("mm", "tr", "act", "tt", "ts", "stt", "cp", "memset", "load", "store", "op"):
            fr = fr.f_back
        o.line = fr.f_lineno
        if dsem is not None:
            dsem.count += 1
            o.dval = 16 * dsem.count
        self.ops.append(o)
        wset = set(id(w) for w in ww)
        for w in ww:
            w.last_w = o
            w.readers = []
        for r in rr:
            if id(r) not in wset:
                if dsem is None:
                    r.readers = [q for q in r.readers if q.dsem is not None or q.eng != eng]
                r.readers.append(o)
        return o

    def emit(self, final_dsems=()):
        nc = self.nc

        def needs(d, o):
            if d.dsem is not None:
                return True
            if d.eng != o.eng:
                return True
            if o.eng == "pe":
                return False
            return d.osz < SAME_ENG_MIN

        for o in self.ops:
            for d in o.deps:
                if d.dsem is None and needs(d, o):
                    d.need_sig = True
        EPOCH = 1800
        cnt = {e: 0 for e in ENGS}
        for o in self.ops:
            if o.dsem is None and o.need_sig:
                o.sig = (cnt[o.eng] // EPOCH, cnt[o.eng] % EPOCH + 1)
                cnt[o.eng] += 1
        self.sig_counts = cnt
        esem = {e: [self.es.enter_context(nc.semaphore(f"es_{e}{i}")) for i in range(cnt[e] // EPOCH + 1)]
                for e in ENGS}
        per = {e: [o for o in self.ops if o.eng == e] for e in ENGS}
        handles = {"pe": "tensor", "act": "scalar", "dve": "vector", "pool": "gpsimd", "sp": "sync"}
        with nc.Block() as block:
            for e in ENGS:

                def body(eng, lst=per[e], e=e):
                    known = {}
                    for o in lst:
                        waits = {}
                        for d in o.deps:
                            if not needs(d, o):
                                continue
                            if d.dsem is not None:
                                key, h, v = ("d", id(d.dsem)), d.dsem.h, (0, d.dval)
                            else:
                                key, h, v = ("e", d.eng), None, d.sig
                            if known.get(key, (0, 0)) >= v:
                                continue
                            if key not in waits or waits[key][1] < v:
                                waits[key] = (h, v)
                        for key, (h, v) in waits.items():
                            if h is None:
                                h = esem[key[1]][v[0]]
                            eng.wait_ge(h, v[1])
                            known[key] = v
                        ins = o.fn(eng)
                        if o.dsem is not None:
                            ins.then_inc(o.dsem.h, 16)
                        elif o.need_sig:
                            ins.then_inc(esem[e][o.sig[0]], 1)
                    if e == "sp":
                        for d in final_dsems:
                            if d.count > 0:
                                eng.wait_ge(d.h, 16 * d.count)

                getattr(block, handles[e])(body)


class Arena:
    GR = 256

    def __init__(self, P, name, nbytes):
        self.nbytes = nbytes
        self.v = P.sb(name, [128, nbytes // 4], F32, nreg=nbytes // self.GR)
        self.off = 0
        self.peak = 0

    def alloc(self, shape, dt):
        n = _prod(shape[1:])
        nb = n * (2 if dt == BF16 else 4)
        nb_al = -(-nb // self.GR) * self.GR
        off = self.off
        self.off += nb_al
        self.peak = max(self.peak, self.off)
        assert self.off <= self.nbytes, f"arena overflow {self.off} > {self.nbytes}"
        ap = self.v.ap[0:shape[0], off // 4: (off + nb_al) // 4]
        if dt == BF16:
            ap = ap.bitcast(BF16)
        ap = ap[:, 0:n]
        if len(shape) == 3:
            ap = ap.rearrange("p (a b) -> p a b", a=shape[1])
        elif len(shape) == 4:
            ap = ap.rearrange("p (a b c) -> p a b c", a=shape[1], b=shape[2])
        return V(ap, self.v.regs[off // self.GR: (off + nb_al) // self.GR])

    def mark(self):
        return self.off

    def reset(self, to=0):
        self.off = to


NLP = 238
NLR = 792
NVEC = 18
NCT = 6 * 128 + 8
NHM = 64 + 64 + 1024 + 3


def build(dbg=None, nlayers=DEPTH, stop=None, wdepth=DEPTH):
    nc = bass.Bass("TRN2", target_bir_lowering=False)
    P = Prog(nc)

    def din(name, shape):
        return nc.dram_tensor(name, list(shape), F32, kind="ExternalInput").ap()

    def dout(name, shape):
        return nc.dram_tensor(name, list(shape), F32, kind="ExternalOutput").ap()

    xT_d = din("xT", [D, T])
    cfm_d = din("cfm", [128, 8])
    vecs_d = din("vecs", [128, NVEC])
    rope_d = din("rope", [128, 2, NCH, DH])
    dftc_d = din("dftC", [T, T])
    dfts_d = din("dftS", [T, T])
    ctab_d = din("ctab", [128, NCT])
    hfeat_d = din("hfeat", [33, 2, T])
    hsm_d = din("hsm", [128, 40 + 256])
    hsvd_d = din("hsvd", [128, 16])
    lp_d = din("lp", [DEPTH, 128, NLP])
    lrep_d = din("lrep", [DEPTH, 128, NLR])
    hmlp_d = din("hmlp", [DEPTH, 65, NHM])
    hyb_d = din("hyb", [DEPTH, 128, 512])
    sret_d = din("sret", [DEPTH, DH, 2, 4, DH])
    scn_d = din("scn", [DEPTH, DH, 2, 4, DH + 1])
    smm_d = din("smm", [DEPTH, 128, 8])
    wmod_d = din("w_mod", [wdepth, D, 6 * D])
    wintm_d = din("w_in_tm", [wdepth, D, 3088])
    winfm_d = din("w_in_fm", [wdepth, D, 768])
    wout_d = din("w_out", [wdepth, D, D])
    wup_d = din("w_up", [wdepth, D, 8 * D])
    wdown_d = din("w_down", [wdepth, 4 * D, D])

    yT_d = dout("yT", [D, T])
    oret_d = dout("o_ret", [4, DEPTH, 2, 4, DH, DH])
    oc_d = dout("o_c", [4, DEPTH, 2, 4, DH, DH])
    on_d = dout("o_n", [4, DEPTH, 2, 4, DH])
    om_d = dout("o_m", [4, DEPTH, 2, 4])

    def mm(out, lhsT, rhs, start, stop, reads, writes):
        P.op("pe", lambda e: e.matmul(out, lhsT, rhs, start=start, stop=stop), reads=reads, writes=writes)

    def tr(out, in_, ident, reads, writes):
        P.op("pe", lambda e: e.transpose(out, in_, ident), reads=reads, writes=writes)

    def act(out, in_, func, reads, writes, scale=None, bias=None):
        kw = {}
        if scale is not None:
            kw["scale"] = scale
        if bias is not None:
            kw["bias"] = bias
        P.op("act", lambda e: e.activation(out=out, in_=in_, func=func, **kw), reads=reads, writes=writes,
             osz=_prod(out.shape[1:]))

    def tt(eng, out, in0, in1, op, reads, writes):
        P.op(eng, lambda e: e.tensor_tensor(out=out, in0=in0, in1=in1, op=op), reads=reads, writes=writes,
             osz=_prod(out.shape[1:]))

    def ts(eng, out, in0, s1, s2, op0, op1, reads, writes):
        if op1 is None:
            P.op(eng, lambda e: e.tensor_scalar(out=out, in0=in0, scalar1=s1, scalar2=None, op0=op0),
                 reads=reads, writes=writes, osz=_prod(out.shape[1:]))
        else:
            P.op(eng, lambda e: e.tensor_scalar(out=out, in0=in0, scalar1=s1, scalar2=s2, op0=op0, op1=op1),
                 reads=reads, writes=writes, osz=_prod(out.shape[1:]))

    def stt(out, in0, scalar, in1, op0, op1, reads, writes):
        P.op("dve", lambda e: e.scalar_tensor_tensor(out=out, in0=in0, scalar=scalar, in1=in1, op0=op0, op1=op1),
             reads=reads, writes=writes, osz=_prod(out.shape[1:]))

    def cp(eng, out, in_, reads, writes):
        if eng == "act":
            P.op("act", lambda e: e.copy(out=out, in_=in_), reads=reads, writes=writes, osz=_prod(out.shape[1:]))
        else:
            P.op(eng, lambda e: e.tensor_copy(out=out, in_=in_), reads=reads, writes=writes,
                 osz=_prod(out.shape[1:]))

    def memset(eng, ap, val, writes):
        P.op(eng, lambda e: e.memset(ap, val), writes=writes)

    ld_sems = {}
    st_sems = {}

    def _sem_for(table, v, pfx):
        k = id(v.regs[0])
        if k not in table:
            table[k] = P.dsem(f"{pfx}{len(table)}")
        return table[k]

    def load(eng, out_v, out_ap, in_ap):
        P.op(eng, lambda e: e.dma_start(out=out_ap, in_=in_ap), writes=[out_v], dsem=_sem_for(ld_sems, out_v, "ld"))

    def store(out_ap, in_v, in_ap, slow=False, eng="sp"):
        ds = _sem_for(st_sems, in_v, "st")
        if slow:
            P.op(eng, lambda e: e.dma_start(out=out_ap, in_=in_ap, allow_slow_non_contiguous=True), reads=[in_v],
                 dsem=ds)
        else:
            P.op(eng, lambda e: e.dma_start(out=out_ap, in_=in_ap), reads=[in_v], dsem=ds)

    dbg_outs = []

    def dump(name, v, ap, shape):
        if dbg is None or name not in dbg:
            return
        o = dout("dbg_" + name, shape)
        dbg_outs.append("dbg_" + name)
        idx = tuple(slice(None) for _ in shape)
        store(o[idx], v, ap, eng=("sp" if ap.dtype == F32 else "pool"))

    pst = [P.ps(f"psum{i}", [128, 1024], F32, nreg=2) for i in range(4)]
    bank_i = [0]
    pair_i = [0]

    held = set()

    def bank():
        while True:
            i = bank_i[0] % 8
            bank_i[0] += 1
            t = pst[i // 2]
            h = i % 2
            if id(t.regs[h]) not in held:
                return V(t.ap[:, h * 512:(h + 1) * 512], [t.regs[h]])

    def hold(v):
        held.add(id(v.regs[0]))

    def release(v):
        held.discard(id(v.regs[0]))

    def pair():
        i = pair_i[0] % 4
        pair_i[0] += 1
        return pst[i]

    x = [P.sb(f"x{k}", [128, T], F32) for k in range(8)]
    hb = [P.sb(f"h{k}", [128, T], BF16) for k in range(8)]
    mixT = P.sb("mixT", [128, 8, T], BF16, nreg=8)

    def mixr(k0, k1):
        return V(None, mixT.regs[k0:k1])

    NSLOT = 6
    slots = [P.sb(f"ring{i}", [128, 8, 256], BF16) for i in range(NSLOT)]
    slot_sem = [P.dsem(f"rs{i}") for i in range(NSLOT)]
    ring_i = [0]

    def ring_load(view, kt, ncols):
        i = ring_i[0] % NSLOT
        ring_i[0] += 1
        s = slots[i]
        P.op("pool", lambda e: e.dma_start(out=s.ap[:, 0:kt, 0:ncols], in_=view), writes=[s], dsem=slot_sem[i])
        return s

    Cm = P.sb("Cm", [128, 8, T], BF16)
    Sm = P.sb("Sm", [128, 8, T], BF16)
    ctab = P.sb("ctab", [128, NCT], F32)
    RP, RN = ctab.ap[:, 0:128], ctab.ap[:, 128:256]
    MLE, MGE = ctab.ap[:, 256:384], ctab.ap[:, 384:512]
    IDX1, IDXR = ctab.ap[:, 512:640], ctab.ap[:, 640:768]
    JREV4, J4 = ctab.ap[:, 768:772], ctab.ap[:, 772:776]
    identf = P.sb("identf", [128, 128], F32)
    identb = P.sb("identb", [128, 128], BF16)
    onesb = P.sb("onesb", [128, 128], BF16)
    onesf = P.sb("onesf", [128, 128], F32)
    rope = P.sb("rope", [128, 2, NCH, DH], F32)
    vecs = P.sb("vecs", [128, NVEC], F32)
    cfm = P.sb("cfm", [128, 8], F32)
    cs = P.sb("cs", [128, 8], BF16)
    hsm = P.sb("hsm", [128, 40 + 256], F32)
    HCOS, HSIN, HNSIN = hsm.ap[:, 0:8], hsm.ap[:, 8:16], hsm.ap[:, 16:24]

    def HTN(d, tt_):
        return hsm.ap[:, 24 + d * 8 + tt_: 25 + d * 8 + tt_]

    hsv = P.sb("hsv", [128, 16], F32)
    hsv05 = P.sb("hsv05", [128, 16], F32)
    hnd = V(hsm.ap[:, 40:296], hsm.regs)
    lp = P.sb("lp", [128, NLP], F32)
    lrep = P.sb("lrep", [128, NLR], F32)
    hmlp = P.sb("hmlp", [65, NHM], F32)
    w3b = P.sb("w3b", [65, 1024], BF16)
    sret = P.sb("sret", [DH, 2, 4, DH], F32)
    scn = P.sb("scn", [DH, 2, 4, DH + 1], F32)
    smm = P.sb("smm", [128, 8], F32)
    modt = P.sb("modt", [128, 48], F32)
    modraw = [P.sb(f"modraw{i}", [128, 48], F32) for i in range(2)]
    gmt = P.sb("gmt", [128, 32], F32)
    cvx = P.sb("cvx", [128, 2 * 38], F32)
    lgt = P.sb("lgt", [128, 8], F32)
    dmask = P.sb("dmask", [128, 4, 128], BF16)
    qdec = P.sb("qdec", [DH, 2, 4, 128], F32)
    kdec = P.sb("kdec", [128, 8], F32)
    cdec = P.sb("cdec", [DH, 8], F32)
    frs = P.sb("frs", [64, 4], F32)

    arena = Arena(P, "arena", 48 * 1024)

    for k in range(8):
        load("sp", x[k], x[k].ap, xT_d[k * 128:(k + 1) * 128, :])
    load("sp", ctab, ctab.ap, ctab_d[:, :])
    load("sp", rope, rope.ap, rope_d[:, :, :, :])
    load("sp", vecs, vecs.ap, vecs_d[:, :])
    load("sp", cfm, cfm.ap, cfm_d[:, :])
    load("sp", hsm, hsm.ap, hsm_d[:, :])
    load("sp", hsv, hsv.ap, hsvd_d[:, :])
    ts("dve", hsv05.ap, hsv.ap, 0.05, None, ALU.mult, None, [hsv], [hsv05])
    dcv = dftc_d.rearrange("(kt p) n -> p kt n", p=128)
    dsv = dfts_d.rearrange("(kt p) n -> p kt n", p=128)
    for hh in range(2):
        load("pool", Cm, Cm.ap[:, :, hh * 512:(hh + 1) * 512], dcv[:, :, hh * 512:(hh + 1) * 512])
        load("pool", Sm, Sm.ap[:, :, hh * 512:(hh + 1) * 512], dsv[:, :, hh * 512:(hh + 1) * 512])
    tt("dve", identf.ap, MLE, MGE, ALU.mult, [ctab], [identf])
    cp("dve", identb.ap, identf.ap, [identf], [identb])
    memset("dve", onesb.ap, 1.0, [onesb])
    memset("dve", onesf.ap, 1.0, [onesf])
    act(cs.ap, cfm.ap, AF.Silu, [cfm], [cs])
    BM = vecs.ap[:, 0:1]
    TWON = vecs.ap[:, 1:2]

    def RMF(c):
        return vecs.ap[:, 2 + c:3 + c]

    def RMB(c):
        return vecs.ap[:, 10 + c:11 + c]

    def rstd_of(src):
        rstd = arena.alloc([128, T], F32)
        sqs = [arena.alloc([128, 512], BF16) for _ in range(3)]
        for half in range(2):
            hs = slice(half * 512, (half + 1) * 512)
            ps = bank()
            for kt in range(8):
                sq = sqs[kt % 3]
                if kt % 2 == 0:
                    act(sq.ap, src[kt].ap[:, hs], AF.Square, [src[kt]], [sq])
                else:
                    tt("dve", sq.ap, src[kt].ap[:, hs], src[kt].ap[:, hs], ALU.mult, [src[kt]], [sq])
                mm(ps.ap, onesb.ap, sq.ap, kt == 0, kt == 7, [onesb, sq], [ps])
            act(rstd.ap[:, hs], ps.ap, AF.Ln, [ps], [rstd], scale=1.0 / D, bias=EPS)
            act(rstd.ap[:, hs], rstd.ap[:, hs], AF.Exp, [rstd], [rstd], scale=-0.5)
        return rstd

    def pre_norm(gcol, shcol):
        m0 = arena.mark()
        rstd = rstd_of(x)
        tmps = [arena.alloc([128, T], F32) for _ in range(2)]
        for kt in range(8):
            tmp = tmps[kt % 2]
            stt(tmp.ap, x[kt].ap, gcol[0](kt), rstd.ap, ALU.mult, ALU.mult, [x[kt], gcol[1], rstd], [tmp])
            act(hb[kt].ap, tmp.ap, AF.Identity, [tmp, shcol[1]], [hb[kt]], bias=shcol[0](kt), scale=1.0)
        arena.reset(m0)

    def post_norm_residual(yo, gcol):
        m0 = arena.mark()
        rstd = rstd_of(yo)
        tmps = [arena.alloc([128, T], F32) for _ in range(2)]
        for kt in range(8):
            tmp = tmps[kt % 2]
            stt(tmp.ap, yo[kt].ap, gcol[0](kt), rstd.ap, ALU.mult, ALU.mult, [yo[kt], gcol[1], rstd], [tmp])
            tt("pool" if kt % 2 else "dve", x[kt].ap, x[kt].ap, tmp.ap, ALU.add, [x[kt], tmp], [x[kt]])
        arena.reset(m0)

    def conv3(src, src_v, cv, col, tapbase, bcol, fixbase, fixcol):
        w0 = lp.ap[:, tapbase[0] + col: tapbase[0] + col + 1]
        w1 = lp.ap[:, tapbase[1] + col: tapbase[1] + col + 1]
        w2 = lp.ap[:, tapbase[2] + col: tapbase[2] + col + 1]
        bb = lp.ap[:, bcol: bcol + 1]
        n0 = cvx.ap[:, fixbase + fixcol: fixbase + fixcol + 1]
        n2 = cvx.ap[:, 38 + fixbase + fixcol: 38 + fixbase + fixcol + 1]
        act(cv.ap, src, AF.Identity, [src_v, lp], [cv], scale=w1, bias=bb)
        stt(cv.ap[:, 1:T], src[:, 0:T - 1], w0, cv.ap[:, 1:T], ALU.mult, ALU.add, [src_v, lp, cv], [cv])
        stt(cv.ap[:, 0:T - 1], src[:, 1:T], w2, cv.ap[:, 0:T - 1], ALU.mult, ALU.add, [src_v, lp, cv], [cv])
        stt(cv.ap[:, 256:T:256], src[:, 255:T - 1:256], n0, cv.ap[:, 256:T:256], ALU.mult, ALU.add,
            [src_v, cvx, cv], [cv])
        stt(cv.ap[:, 255:T - 1:256], src[:, 256:T:256], n2, cv.ap[:, 255:T - 1:256], ALU.mult, ALU.add,
            [src_v, cvx, cv], [cv])

    def head_post(pso_ap, pso_v, gcols, c, k0, sb_src=False):
        m0 = arena.mark()
        sq = arena.alloc([128, 4, DH], F32)
        ss = arena.alloc([128, 4], F32)
        on = sq
        mx = arena.alloc([128, 384], BF16)
        act(sq.ap, pso_ap, AF.Square, [pso_v], [sq])
        P.op("dve", lambda e: e.tensor_reduce(out=ss.ap, in_=sq.ap, axis=AX.X, op=ALU.add), reads=[sq], writes=[ss])
        act(ss.ap, ss.ap, AF.Ln, [ss], [ss], scale=1.0 / DH, bias=EPS)
        act(ss.ap, ss.ap, AF.Exp, [ss], [ss], scale=-0.5)
        tt("dve", mx.ap.rearrange("p (a b) -> p a b", a=4), pso_ap, ss.ap.unsqueeze(2).broadcast_to([128, 4, DH]),
           ALU.mult, [pso_v, ss], [mx])
        pt = bank()
        ptb = pt.ap.bitcast(BF16)
        for j in range(3):
            tr(ptb[:, j * 128:(j + 1) * 128], mx.ap[:, j * 128:(j + 1) * 128], identb.ap, [mx, identb], [pt])
        cp("act", mixT.ap[:, k0:k0 + 3, c * 128:(c + 1) * 128],
           ptb[:, 0:384].rearrange("p (a b) -> p a b", a=3), [pt], [mixr(k0, k0 + 3)])
        arena.reset(m0)

    def tm_proj(l, col0, ncols, evac):
        wv = wintm_d[l].rearrange("(kt p) n -> p kt n", p=128)
        half_ = ncols // 2
        pans = [(0, half_), (half_, ncols - half_)]
        ss = [ring_load(wv[:, :, col0 + o_: col0 + o_ + n_], 8, n_) for (o_, n_) in pans]
        for c in range(NCH):
            ps = bank()
            for (o_, n_), s in zip(pans, ss):
                for kt in range(8):
                    mm(ps.ap[:, o_:o_ + n_], hb[kt].ap[:, c * 128:(c + 1) * 128], s.ap[:, kt, 0:n_], kt == 0, kt == 7,
                       [hb[kt], s], [ps])
            evac(c, ps)

    def fm_proj(wview, ntiles, evac):
        s = None
        for f in range(ntiles):
            if f % 2 == 0:
                n_ = min(256, (ntiles - f) * 128)
                s = ring_load(wview[:, :, f * 128: f * 128 + n_], 8, n_)
            j = f % 2
            for half in range(2):
                ps = bank()
                for kt in range(8):
                    mm(ps.ap, s.ap[:, kt, j * 128:(j + 1) * 128], hb[kt].ap[:, half * 512:(half + 1) * 512],
                       kt == 0, kt == 7, [s, hb[kt]], [ps])
                evac(f, half, ps)

    def transpose_heads(src_tm, c, dsts):
        pt = bank()
        ptb = pt.ap.bitcast(BF16)
        for i, s in enumerate(src_tm):
            for h in range(4):
                tr(ptb[0:DH, i * 512 + h * 128: i * 512 + (h + 1) * 128], s.ap[:, c, h * DH:(h + 1) * DH], identb.ap,
                   [s, identb], [pt])
        for i, d in enumerate(dsts):
            cp("act", d.ap,
               ptb[0:DH, i * 512:(i + 1) * 512].rearrange("p (a b) -> p a b", a=4), [pt], [d])

    mod_gen = [iter(())]

    def mod_panels(l):
        wmv = wmod_d[l].rearrange("(kt p) n -> p kt n", p=128)
        dst = modraw[l % 2]
        for p_ in range(24):
            s = ring_load(wmv[:, :, p_ * 256:(p_ + 1) * 256], 8, 256)
            ps = bank()
            for j in range(2):
                for kt in range(8):
                    mm(ps.ap[:, j:j + 1], s.ap[:, kt, j * 128:(j + 1) * 128], cs.ap[:, kt:kt + 1], kt == 0, kt == 7,
                       [s, cs], [ps])
            cp("act", dst.ap[:, p_ * 2:p_ * 2 + 2], ps.ap[:, 0:2], [ps], [dst])
            yield p_

    def mod_step():
        next(mod_gen[0], None)

    def layer(l):
        arena.reset(0)
        load("sp", lp, lp.ap, lp_d[l])
        load("sp", lrep, lrep.ap, lrep_d[l])
        load("sp", hmlp, hmlp.ap, hmlp_d[l])
        load("sp", sret, sret.ap, sret_d[l])
        load("sp", scn, scn.ap, scn_d[l])
        load("sp", smm, smm.ap, smm_d[l])

        if l == 0:
            for _ in mod_panels(0):
                pass
        else:
            for _ in mod_gen[0]:
                pass
        mod_gen[0] = mod_panels(l + 1) if l + 1 < nlayers else iter(())
        tt("dve", modt.ap, modraw[l % 2].ap, lp.ap[:, 32:80], ALU.add, [modraw[l % 2], lp], [modt])
        stt(gmt.ap[:, 0:8], modt.ap[:, 8:16], 1.0, lp.ap[:, 0:8], ALU.add, ALU.mult, [modt, lp], [gmt])
        tt("dve", gmt.ap[:, 8:16], modt.ap[:, 16:24], lp.ap[:, 8:16], ALU.mult, [modt, lp], [gmt])
        stt(gmt.ap[:, 16:24], modt.ap[:, 32:40], 1.0, lp.ap[:, 16:24], ALU.add, ALU.mult, [modt, lp], [gmt])
        tt("dve", gmt.ap[:, 24:32], modt.ap[:, 40:48], lp.ap[:, 24:32], ALU.mult, [modt, lp], [gmt])
        ts("dve", cvx.ap[:, 0:32], lp.ap[:, 80:112], BM, -1.0, ALU.mult, ALU.mult, [lp, vecs], [cvx])
        ts("dve", cvx.ap[:, 32:38], lp.ap[:, 208:214], BM, -1.0, ALU.mult, ALU.mult, [lp, vecs], [cvx])
        ts("dve", cvx.ap[:, 38:70], lp.ap[:, 144:176], BM, -1.0, ALU.mult, ALU.mult, [lp, vecs], [cvx])
        ts("dve", cvx.ap[:, 70:76], lp.ap[:, 220:226], BM, -1.0, ALU.mult, ALU.mult, [lp, vecs], [cvx])

        DL = lrep.ap[:, 784:792]
        act(lgt.ap, DL, AF.Exp, [lrep], [lgt], scale=-1.0)
        act(lgt.ap, lgt.ap, AF.Ln, [lgt], [lgt], bias=1.0, scale=1.0)
        ts("dve", lgt.ap, lgt.ap, -1.0, None, ALU.mult, None, [lgt], [lgt])
        m0 = arena.mark()
        ta = arena.alloc([128, 128], F32)
        tb = arena.alloc([128, 128], F32)
        for h in range(4):
            act(ta.ap, RP, AF.Exp, [ctab, lgt], [ta], scale=lgt.ap[:, h:h + 1])
            tt("pool", ta.ap, ta.ap, MLE, ALU.mult, [ta, ctab], [ta])
            act(tb.ap, RN, AF.Exp, [ctab, lgt], [tb], scale=lgt.ap[:, 4 + h:5 + h])
            tt("pool", tb.ap, tb.ap, MGE, ALU.mult, [tb, ctab], [tb])
            tt("dve", ta.ap, ta.ap, tb.ap, ALU.add, [ta, tb], [ta])
            ts("dve", dmask.ap[:, h, :], ta.ap, SCL, None, ALU.mult, None, [ta], [dmask])
            act(qdec.ap[:, 0, h, :], IDX1[0:DH, :], AF.Exp, [ctab, lgt], [qdec], scale=lgt.ap[0:DH, h:h + 1])
            act(qdec.ap[:, 1, h, :], IDXR[0:DH, :], AF.Exp, [ctab, lgt], [qdec], scale=lgt.ap[0:DH, 4 + h:5 + h])
        tk = arena.alloc([128, 8], F32)
        tt("dve", tk.ap[:, 0:4], JREV4, lgt.ap[:, 0:4], ALU.mult, [ctab, lgt], [tk])
        tt("dve", tk.ap[:, 4:8], J4, lgt.ap[:, 4:8], ALU.mult, [ctab, lgt], [tk])
        act(kdec.ap, tk.ap, AF.Exp, [tk], [kdec])
        ts("dve", kdec.ap, kdec.ap, SCL, None, ALU.mult, None, [kdec], [kdec])
        act(cdec.ap, lgt.ap[0:DH, :], AF.Exp, [lgt], [cdec], scale=128.0)
        arena.reset(m0)
        FR, B1, B2 = hmlp.ap[0:64, 1152:1153], hmlp.ap[0:64, 1153:1154], hmlp.ap[0:64, 1154:1155]
        ts("dve", frs.ap[:, 0:1], FR, 1.0 / 3.0, None, ALU.mult, None, [hmlp], [frs])
        tt("dve", frs.ap[:, 1:2], frs.ap[:, 0:1], B1, ALU.mult, [frs, hmlp], [frs])
        tt("dve", frs.ap[:, 2:3], frs.ap[:, 0:1], B2, ALU.mult, [frs, hmlp], [frs])
        cp("dve", w3b.ap, hmlp.ap[:, 128:1152], [hmlp], [w3b])

        if stop == 'mod':
            return
        pre_norm((lambda kt: gmt.ap[:, kt:kt + 1], gmt), (lambda kt: modt.ap[:, kt:kt + 1], modt))
        dump("h1", hb[0], hb[0].ap, [128, T])

        if stop == 'norm1':
            return
        am = arena.mark()
        q_tm = arena.alloc([128, NCH, 384], BF16)
        k_tm = arena.alloc([128, NCH, 384], BF16)
        v_tm = arena.alloc([128, NCH, 384], BF16)
        gts = arena.alloc([128, NCH, 16], F32)
        rt = [arena.alloc([128, 4, DH], F32) for _ in range(4)]

        def rope_evac(dst):
            def f(c, ps):
                xv = ps.ap[:, 0:384].rearrange("p (a b) -> p a b", a=4)
                cos2 = rope.ap[:, 0, c, :].unsqueeze(1).broadcast_to([128, 4, DH])
                t1, t2 = rt[(c % 2) * 2], rt[(c % 2) * 2 + 1]
                tt("dve", t1.ap, xv, cos2, ALU.mult, [ps, rope], [t1])
                xsw = ps.ap[:, 0:384].rearrange("p (a h b) -> p a h b", a=4, h=2)[:, :, ::-1, :]
                sn4 = rope.ap[:, 1, c, :].rearrange("p (h b) -> p h b", h=2).unsqueeze(1).broadcast_to([128, 4, 2, 48])
                tt("dve", t2.ap.rearrange("p a (h b) -> p a h b", h=2), xsw, sn4, ALU.mult, [ps, rope], [t2])
                tt("pool", dst.ap[:, c, :].rearrange("p (a b) -> p a b", a=4), t1.ap, t2.ap, ALU.add, [t1, t2], [dst])
            return f

        def q_evac(c, ps):
            rope_evac(q_tm)(c, ps)
            tt("dve", gates_p.ap[:, c, :], ps.ap[:, 384:400], lrep.ap[:, 768:784], ALU.add, [ps, lrep], [gates_p])

        tm_proj(l, 0, 400, q_evac)
        tm_proj(l, 400, 384, rope_evac(k_tm))
        tm_proj(l, 784, 384, lambda c, ps: cp("act", v_tm.ap[:, c, :], ps.ap[:, 0:384], [ps], [v_tm]))
        dump("q_tm", q_tm, q_tm.ap[:, 0, :], [128, 384])
        if stop == 'ret_proj':
            return

        Sb_bf = arena.alloc([DH, NCH, 4, DH], BF16)
        Sf_bf = [arena.alloc([DH, 4, DH], BF16) for _ in range(2)]
        Scur = arena.alloc([DH, 4, DH], F32)
        Stmp = arena.alloc([DH, 4, DH], F32)
        Sout = [arena.alloc([DH, 4, DH], F32) for _ in range(2)]
        kd = arena.alloc([128, 4, DH], BF16)
        qTc = arena.alloc([DH, 4, 128], BF16)
        kTc = arena.alloc([DH, 4, 128], BF16)
        qdf = arena.alloc([DH, 4, 128], BF16)
        qdb = arena.alloc([DH, 4, 128], BF16)
        Pc = arena.alloc([128, 4, 128], BF16)
        so_i = [0]

        def ret_scan_step(dirn, c, psA):
            cd = cdec.ap[:, dirn * 4:(dirn + 1) * 4].unsqueeze(2).broadcast_to([DH, 4, DH])
            tt("dve", Stmp.ap, Scur.ap, cd, ALU.mult, [Scur, cdec], [Stmp])
            so = Sout[so_i[0] % 2]
            so_i[0] += 1
            tt("dve", so.ap, Stmp.ap, psA.ap[0:DH, 0:384].rearrange("p (a b) -> p a b", a=4), ALU.add, [Stmp, psA], [so])
            is_end = (c % 2 == 1) if dirn == 0 else (c % 2 == 0)
            if is_end:
                store(oret_d[c // 2, l, dirn].rearrange("h d e -> d h e"), so, so.ap)
            return so

        cp("dve", Scur.ap, sret.ap[:, 1], [sret], [Scur])
        cp("act", Sb_bf.ap[:, NCH - 1], Scur.ap, [Scur], [Sb_bf])
        for c in range(NCH - 1, -1, -1):
            mod_step()
            tt("dve", kd.ap, k_tm.ap[:, c, :].rearrange("p (a b) -> p a b", a=4),
               kdec.ap[:, 4:8].unsqueeze(2).broadcast_to([128, 4, DH]), ALU.mult, [k_tm, kdec], [kd])
            psA = bank()
            for h in range(4):
                mm(psA.ap[0:DH, h * DH:(h + 1) * DH], kd.ap[:, h, :], v_tm.ap[:, c, h * DH:(h + 1) * DH], True, True,
                   [kd, v_tm], [psA])
            so = ret_scan_step(1, c, psA)
            if c > 0:
                act(Scur.ap, so.ap, AF.Identity, [so, vecs], [Scur], scale=RMB(c - 1)[0:DH, :])
                cp("act", Sb_bf.ap[:, c - 1], Scur.ap, [Scur], [Sb_bf])
        if stop == 'ret_passA':
            return
        ret_pending = [None]
        cp("dve", Scur.ap, sret.ap[:, 0], [sret], [Scur])
        cp("act", Sf_bf[0].ap, Scur.ap, [Scur], [Sf_bf[0]])
        for c in range(NCH):
            if stop == f'retB_c{c}':
                return
            b2 = c % 2
            mod_step()
            transpose_heads([q_tm, k_tm], c, [qTc, kTc])
            if stop == f'retB_tr{c}':
                return
            tt("dve", qdf.ap, qTc.ap, qdec.ap[:, 0], ALU.mult, [qTc, qdec], [qdf])
            tt("pool", qdb.ap, qTc.ap, qdec.ap[:, 1], ALU.mult, [qTc, qdec], [qdb])
            pss = bank()
            for h in range(4):
                mm(pss.ap[:, h * 128:(h + 1) * 128], kTc.ap[:, h, :], qTc.ap[:, h, :], True, True, [kTc, qTc], [pss])
            tt("dve", Pc.ap, pss.ap.rearrange("p (a b) -> p a b", a=4), dmask.ap, ALU.mult, [pss, dmask], [Pc])
            if stop == f'retB_pc{c}':
                return
            tt("dve", kd.ap, k_tm.ap[:, c, :].rearrange("p (a b) -> p a b", a=4),
               kdec.ap[:, 0:4].unsqueeze(2).broadcast_to([128, 4, DH]), ALU.mult, [k_tm, kdec], [kd])
            psA = bank()
            for h in range(4):
                mm(psA.ap[0:DH, h * DH:(h + 1) * DH], kd.ap[:, h, :], v_tm.ap[:, c, h * DH:(h + 1) * DH], True, True,
                   [kd, v_tm], [psA])
            pso = bank()
            hold(pso)
            sfb = Sf_bf[b2]
            for h in range(4):
                o_ = pso.ap[:, h * DH:(h + 1) * DH]
                mm(o_, Pc.ap[:, h, :], v_tm.ap[:, c, h * DH:(h + 1) * DH], True, False, [Pc, v_tm], [pso])
                mm(o_, qdf.ap[:, h, :], sfb.ap[:, h, :], False, False, [qdf, sfb], [pso])
                mm(o_, qdb.ap[:, h, :], Sb_bf.ap[:, c, h, :], False, True, [qdb, Sb_bf], [pso])
            if stop == f'retB_pso{c}':
                return
            so = ret_scan_step(0, c, psA)
            if stop == f'retB_scan{c}':
                return
            if c < NCH - 1:
                act(Scur.ap, so.ap, AF.Identity, [so, vecs], [Scur], scale=RMF(c + 1)[0:DH, :])
                cp("act", Sf_bf[1 - b2].ap, Scur.ap, [Scur], [Sf_bf[1 - b2]])
            if ret_pending[0] is not None:
                ret_pending[0]()
            ret_pending[0] = (lambda c=c, pso=pso: (release(pso),
                                                    head_post(pso.ap[:, 0:384].rearrange("p (a b) -> p a b", a=4),
                                                              pso, 0, c, 0)))
        ret_pending[0]()
        arena.reset(am)

        if stop == 'ret':
            return
        am = arena.mark()
        mq_tm = arena.alloc([128, NCH, 384], BF16)
        mk_tm = arena.alloc([128, NCH, 384], BF16)
        mv1 = arena.alloc([128, NCH, 4, DH + 1], BF16)
        memset("pool", mv1.ap[:, :, :, DH:DH + 1], 1.0, [mv1])
        tm_proj(l, 1552, 384, lambda c, ps: cp("act", mq_tm.ap[:, c, :], ps.ap[:, 0:384], [ps], [mq_tm]))
        tm_proj(l, 1936, 384, lambda c, ps: P.op(
            "act", lambda e, c=c, ps=ps: e.mul(out=mk_tm.ap[:, c, :], in_=ps.ap[:, 0:384], mul=SCL), reads=[ps],
            writes=[mk_tm]))
        tm_proj(l, 2320, 384, lambda c, ps: cp("act", mv1.ap[:, c, :, 0:DH],
                                               ps.ap[:, 0:384].rearrange("p (a b) -> p a b", a=4), [ps], [mv1]))
        g5 = gates_p.ap.rearrange("p c (d k h) -> p c d k h", d=2, k=2)
        ipre = g5[:, :, :, 0, :]
        fpre = g5[:, :, :, 1, :]
        lf = arena.alloc([128, NCH, 2, 4], F32)
        wg = arena.alloc([128, NCH, 2, 4], F32)
        ew = arena.alloc([128, NCH, 2, 4], F32)
        eb = arena.alloc([128, NCH, 2, 4], F32)
        ebt = arena.alloc([DH, NCH, 8], F32)
        btR = arena.alloc([DH, NCH, 8], F32)
        wmR = arena.alloc([DH, NCH, 8], F32)
        act(lf.ap, fpre, AF.Exp, [gates_p], [lf], scale=-1.0)
        act(lf.ap, lf.ap, AF.Ln, [lf], [lf], bias=1.0, scale=1.0)
        ts("dve", lf.ap, lf.ap, -1.0, None, ALU.mult, None, [lf], [lf])
        psb = bank()
        psbt = bank()
        for c in range(NCH):
            mm(psb.ap[:, c * 8:c * 8 + 4], MLE, lf.ap[:, c, 0, :], True, True, [ctab, lf], [psb])
            mm(psb.ap[:, c * 8 + 4:c * 8 + 8], MGE, lf.ap[:, c, 1, :], True, True, [ctab, lf], [psb])
            mm(psbt.ap[0:DH, c * 8:(c + 1) * 8], onesf.ap[:, 0:DH], lf.ap[:, c].rearrange("p a b -> p (a b)"), True,
               True, [onesf, lf], [psbt])
        bps = psb.ap[:, 0:64].rearrange("p (c d h) -> p c d h", c=NCH, d=2)
        cp("dve", eb.ap, bps, [psb], [eb])
        tt("dve", wg.ap, ipre, eb.ap, ALU.subtract, [gates_p, eb], [wg])
        act(ew.ap, wg.ap, AF.Exp, [wg], [ew])
        act(eb.ap, eb.ap, AF.Exp, [eb], [eb], scale=-1.0)
        cp("dve", btR.ap, psbt.ap[0:DH, 0:64].rearrange("p (c n) -> p c n", c=NCH), [psbt], [btR])
        act(ebt.ap, btR.ap, AF.Exp, [btR], [ebt])
        pw = bank()
        tr(pw.ap[0:64, 0:128], wg.ap.rearrange("p c d h -> p (c d h)"), identf.ap, [wg, identf], [pw])
        wmc = arena.alloc([64, 1], F32)
        P.op("dve", lambda e: e.tensor_reduce(out=wmc.ap, in_=pw.ap[0:64, 0:128], axis=AX.X, op=ALU.max), reads=[pw],
             writes=[wmc])
        dg = arena.alloc([64, 64], F32)
        ts("dve", dg.ap, identf.ap[0:64, 0:64], wmc.ap[:, 0:1], None, ALU.mult, None, [identf, wmc], [dg])
        pw2 = bank()
        mm(pw2.ap[0:DH, 0:64], onesf.ap[0:64, 0:DH], dg.ap, True, True, [onesf, dg], [pw2])
        cp("dve", wmR.ap, pw2.ap[0:DH, 0:64].rearrange("p (c n) -> p c n", c=NCH), [pw2], [wmR])

        Gb_bf = arena.alloc([DH, NCH, 4, DH + 1], BF16)
        Gf_bf = [arena.alloc([DH, 4, DH + 1], BF16) for _ in range(2)]
        Gcur = arena.alloc([DH, 4, DH + 1], F32)
        Gtmp = arena.alloc([DH, 4, DH + 1], F32)
        Gout = arena.alloc([DH, 4, DH + 1], F32)
        mcur = arena.alloc([DH, 4], F32)
        mtmp = arena.alloc([DH, 4], F32)
        mout = [arena.alloc([DH, 4], F32) for _ in range(2)]
        emo = arena.alloc([DH, 4], F32)
        em0 = arena.alloc([DH, 8], F32)
        V2 = arena.alloc([128, 2, 4, DH + 1], BF16)
        Vf = V(V2.ap[:, 0], V2.regs)
        Vb = V(V2.ap[:, 1], V2.regs)
        mqTc = arena.alloc([DH, 4, 128], BF16)
        mkTc = arena.alloc([DH, 4, 128], BF16)
        PT2 = arena.alloc([128, 2, 4, 128], BF16)
        PTf = V(PT2.ap[:, 0], PT2.regs)
        PTb = V(PT2.ap[:, 1], PT2.regs)
        dn = arena.alloc([128, 2, 4], F32)
        hd = [arena.alloc([128, 4, DH], F32) for _ in range(2)]
        go_i = [0]
        act(em0.ap, smm.ap[0:DH, :], AF.Exp, [smm], [em0])

        def ml_scan_step(dirn, c, psU):
            tt("dve", Gtmp.ap, Gcur.ap, psU.ap[0:DH, 0:388].rearrange("p (a b) -> p a b", a=4), ALU.add, [Gcur, psU],
               [Gtmp])
            go, gs = Gout, Gtmp
            mo = mout[go_i[0] % 2]
            go_i[0] += 1
            tt("dve", go.ap, Gtmp.ap, ebt.ap[:, c, dirn * 4:(dirn + 1) * 4].unsqueeze(2).broadcast_to([DH, 4, DH + 1]),
               ALU.mult, [Gtmp, ebt], [go])
            tt("dve", mtmp.ap, mcur.ap, wmR.ap[:, c, dirn * 4:(dirn + 1) * 4], ALU.max, [mcur, wmR], [mtmp])
            tt("dve", mo.ap, mtmp.ap, btR.ap[:, c, dirn * 4:(dirn + 1) * 4], ALU.add, [mtmp, btR], [mo])
            is_end = (c % 2 == 1) if dirn == 0 else (c % 2 == 0)
            if is_end:
                act(emo.ap, mo.ap, AF.Exp, [mo], [emo], scale=-1.0)
                tt("dve", gs.ap, go.ap, emo.ap.unsqueeze(2).broadcast_to([DH, 4, DH + 1]), ALU.mult, [go, emo], [gs])
                store(oc_d[c // 2, l, dirn].rearrange("h d e -> d h e"), gs, gs.ap[:, :, 0:DH])
                store(on_d[c // 2, l, dirn].rearrange("h d -> d h"), gs, gs.ap[:, :, DH], slow=True)
                store(om_d[c // 2, l, dirn:dirn + 1, :], mo, mo.ap[0:1, :])
            return go, mo

        tt("dve", Gcur.ap, scn.ap[:, 1], em0.ap[:, 4:8].unsqueeze(2).broadcast_to([DH, 4, DH + 1]), ALU.mult,
           [scn, em0], [Gcur])
        cp("act", Gb_bf.ap[:, NCH - 1], Gcur.ap, [Gcur], [Gb_bf])
        cp("dve", mcur.ap, smm.ap[0:DH, 4:8], [smm], [mcur])
        for c in range(NCH - 1, -1, -1):
            mod_step()
            tt("dve", Vb.ap, mv1.ap[:, c], ew.ap[:, c, 1, :].unsqueeze(2).broadcast_to([128, 4, DH + 1]), ALU.mult,
               [mv1, ew], [Vb])
            psU = bank()
            for h in range(4):
                mm(psU.ap[0:DH, h * 97:(h + 1) * 97], mk_tm.ap[:, c, h * DH:(h + 1) * DH], Vb.ap[:, h, :], True, True,
                   [mk_tm, Vb], [psU])
            go, mo = ml_scan_step(1, c, psU)
            if c > 0:
                act(Gcur.ap, go.ap, AF.Identity, [go, vecs], [Gcur], scale=RMB(c - 1)[0:DH, :])
                cp("act", Gb_bf.ap[:, c - 1], Gcur.ap, [Gcur], [Gb_bf])
                act(mcur.ap, mo.ap, AF.Identity, [mo, vecs], [mcur], scale=RMB(c - 1)[0:DH, :])
        ml_pending = [None]
        tt("dve", Gcur.ap, scn.ap[:, 0], em0.ap[:, 0:4].unsqueeze(2).broadcast_to([DH, 4, DH + 1]), ALU.mult,
           [scn, em0], [Gcur])
        cp("act", Gf_bf[0].ap, Gcur.ap, [Gcur], [Gf_bf[0]])
        cp("dve", mcur.ap, smm.ap[0:DH, 0:4], [smm], [mcur])
        for c in range(NCH):
            b2 = c % 2
            mod_step()
            transpose_heads([mq_tm, mk_tm], c, [mqTc, mkTc])
            pss = bank()
            for h in range(4):
                mm(pss.ap[:, h * 128:(h + 1) * 128], mkTc.ap[:, h, :], mqTc.ap[:, h, :], True, True, [mkTc, mqTc], [pss])
            s4 = pss.ap.rearrange("p (a b) -> p a b", a=4)
            msk2 = ctab.ap[:, 256:512].rearrange("p (d i) -> p d i", d=2).unsqueeze(2).broadcast_to([128, 2, 4, 128])
            tt("dve", PT2.ap, s4.unsqueeze(1).broadcast_to([128, 2, 4, 128]), msk2, ALU.mult, [pss, ctab], [PT2])
            tt("dve", V2.ap, mv1.ap[:, c].unsqueeze(1).broadcast_to([128, 2, 4, DH + 1]),
               ew.ap[:, c].unsqueeze(3).broadcast_to([128, 2, 4, DH + 1]), ALU.mult, [mv1, ew], [V2])
            psU = bank()
            for h in range(4):
                mm(psU.ap[0:DH, h * 97:(h + 1) * 97], mk_tm.ap[:, c, h * DH:(h + 1) * DH], Vf.ap[:, h, :], True, True,
                   [mk_tm, Vf], [psU])
            psN = [bank(), bank()]
            hold(psN[0])
            hold(psN[1])
            gfb = Gf_bf[b2]
            for h in range(4):
                mm(psN[0].ap[:, h * 97:(h + 1) * 97], PTf.ap[:, h, :], Vf.ap[:, h, :], True, False, [PTf, Vf], [psN[0]])
                mm(psN[0].ap[:, h * 97:(h + 1) * 97], mqTc.ap[:, h, :], gfb.ap[:, h, :], False, True, [mqTc, gfb],
                   [psN[0]])
            for h in range(4):
                mm(psN[1].ap[:, h * 97:(h + 1) * 97], PTb.ap[:, h, :], Vb.ap[:, h, :], True, False, [PTb, Vb], [psN[1]])
                mm(psN[1].ap[:, h * 97:(h + 1) * 97], mqTc.ap[:, h, :], Gb_bf.ap[:, c, h, :], False, True,
                   [mqTc, Gb_bf], [psN[1]])
            go, mo = ml_scan_step(0, c, psU)
            if c < NCH - 1:
                act(Gcur.ap, go.ap, AF.Identity, [go, vecs], [Gcur], scale=RMF(c + 1)[0:DH, :])
                cp("act", Gf_bf[1 - b2].ap, Gcur.ap, [Gcur], [Gf_bf[1 - b2]])
                act(mcur.ap, mo.ap, AF.Identity, [mo, vecs], [mcur], scale=RMF(c + 1)[0:DH, :])
            def ml_tail(c=c, psN=psN):
                release(psN[0])
                release(psN[1])
                n4s = [psN[dirn].ap[:, 0:388].rearrange("p (a b) -> p a b", a=4) for dirn in range(2)]
                for dirn in range(2):
                    act(dn.ap[:, dirn, :], n4s[dirn][:, :, DH], AF.Abs, [psN[dirn]], [dn])
                tt("dve", dn.ap, dn.ap, eb.ap[:, c], ALU.max, [dn, eb], [dn])
                P.op("dve", lambda e: e.reciprocal(out=dn.ap, in_=dn.ap), reads=[dn], writes=[dn])
                for dirn in range(2):
                    tt("dve", hd[dirn].ap, n4s[dirn][:, :, 0:DH],
                       dn.ap[:, dirn, :].unsqueeze(2).broadcast_to([128, 4, DH]), ALU.mult, [psN[dirn], dn], [hd[dirn]])
                tt("dve", hd[0].ap, hd[0].ap, hd[1].ap, ALU.add, [hd[0], hd[1]], [hd[0]])
                head_post(hd[0].ap, hd[0], 384, c, 5, sb_src=True)

            if ml_pending[0] is not None:
                ml_pending[0]()
            ml_pending[0] = ml_tail
        ml_pending[0]()
        arena.reset(am)

        if stop == 'ml':
            return
        am = arena.mark()
        gtmp = [arena.alloc([128, 512], BF16) for _ in range(2)]
        wtv = wintm_d[l].rearrange("(kt p) n -> p kt n", p=128)

        def gate_evac(func, k0, gcol0):
            def f(fi, half, ps):
                g_ = gtmp[(fi * 2 + half) % 2]
                hs_ = slice(half * 512, (half + 1) * 512)
                act(g_.ap, ps.ap, func, [ps], [g_])
                stt(mixT.ap[:, k0 + fi, hs_], g_.ap, lp.ap[:, gcol0 + fi:gcol0 + fi + 1], mixT.ap[:, k0 + fi, hs_],
                    ALU.mult, ALU.mult, [mixr(k0 + fi, k0 + fi + 1), g_, lp], [mixr(k0 + fi, k0 + fi + 1)])
            return f

        fm_proj(wtv[:, :, 1168:1552], 3, gate_evac(AF.Silu, 0, 232))
        fm_proj(wtv[:, :, 2704:3088], 3, gate_evac(AF.Sigmoid, 5, 235))
        arena.reset(am)
        dump("mix_ret", mixr(0, 1), mixT.ap[:, 0, :], [128, T])
        dump("mix_ml", mixr(5, 6), mixT.ap[:, 5, :], [128, T])

        if stop == 'gates':
            return
        am = arena.mark()
        hx2 = arena.alloc([128, 2, T], F32)
        hv_tm = arena.alloc([128, NCH, 256], BF16)
        hx1_tm = arena.alloc([128, NCH, 256], F32)
        m1 = arena.mark()
        hv = arena.alloc([128, 2, T], BF16)
        hx1 = arena.alloc([128, 2, T], F32)
        hrs = [arena.alloc([128, T], F32) for _ in range(2)]
        cvs_ = [arena.alloc([128, T], F32) for _ in range(2)]
        wfv = winfm_d[l].rearrange("(kt p) n -> p kt n", p=128)

        def hy_evac(f6, half, ps):
            hr, cv = hrs[f6 % 2], cvs_[f6 % 2]
            cp("act", hr.ap[:, half * 512:(half + 1) * 512], ps.ap, [ps], [hr])
            if half == 1:
                conv3(hr.ap, hr, cv, f6, (208, 214, 220), 226 + f6, 32, f6)
                dst = [hv, hv, hx1, hx1, hx2, hx2][f6]
                cp("pool", dst.ap[:, f6 % 2, :], cv.ap, [cv], [dst])

        fm_proj(wfv, 6, hy_evac)
        dump("hx2", hx2, hx2.ap[:, 0, :], [128, T])
        for c4 in range(2):
            pt = bank()
            ptb = pt.ap.bitcast(BF16)
            for cc in range(4):
                c = c4 * 4 + cc
                for j in range(2):
                    tr(ptb[:, cc * 256 + j * 128: cc * 256 + (j + 1) * 128], hv.ap[:, j, c * 128:(c + 1) * 128],
                       identb.ap, [hv, identb], [pt])
            cp("act", hv_tm.ap[:, c4 * 4:(c4 + 1) * 4, :], ptb.rearrange("p (a b) -> p a b", a=4), [pt], [hv_tm])
        for c2 in range(4):
            pt = bank()
            for cc in range(2):
                c = c2 * 2 + cc
                for j in range(2):
                    tr(pt.ap[:, cc * 256 + j * 128: cc * 256 + (j + 1) * 128], hx1.ap[:, j, c * 128:(c + 1) * 128],
                       identf.ap, [hx1, identf], [pt])
            cp("dve", hx1_tm.ap[:, c2 * 2:(c2 + 1) * 2, :], pt.ap.rearrange("p (a b) -> p a b", a=2), [pt], [hx1_tm])
        arena.reset(m1)
        h2aug = arena.alloc([65, 2, T], BF16)
        biasA = arena.alloc([128, 512], F32)
        hyb = arena.alloc([128, 512], F32)
        load("sp", hyb, hyb.ap, hyb_d[l])
        ts("dve", biasA.ap, hyb.ap, TWON, None, ALU.mult, None, [hyb, vecs], [biasA])
        m2 = arena.mark()
        hfeat = arena.alloc([33, 2, T], F32)
        load("sp", hfeat, hfeat.ap, hfeat_d[:, :, :])
        memset("dve", h2aug.ap[64:65, :, :], 1.0, [h2aug])
        hm1 = arena.alloc([64, T], F32)
        hs1 = arena.alloc([64, 512], F32)
        hs2 = arena.alloc([64, 512], F32)

        def sin3(ps, bias_col, out_ap, out_v):
            act(hs1.ap, ps.ap[0:64, :], AF.Sin, [ps, frs], [hs1], scale=frs.ap[:, 0:1], bias=bias_col)
            tt("dve", hs2.ap, hs1.ap, hs1.ap, ALU.mult, [hs1], [hs2])
            ts("dve", hs2.ap, hs2.ap, -4.0, 3.0, ALU.mult, ALU.add, [hs2], [hs2])
            tt("dve", out_ap, hs1.ap, hs2.ap, ALU.mult, [hs1, hs2], [out_v])

        for d_ in range(2):
            for half in range(2):
                hs_ = slice(half * 512, (half + 1) * 512)
                ps = bank()
                mm(ps.ap[0:64, :], hmlp.ap[0:33, 0:64], hfeat.ap[:, d_, hs_], True, True, [hmlp, hfeat], [ps])
                sin3(ps, frs.ap[:, 1:2], hm1.ap[:, hs_], hm1)
            for half in range(2):
                hs_ = slice(half * 512, (half + 1) * 512)
                ps = bank()
                mm(ps.ap[0:64, :], hmlp.ap[0:64, 64:128], hm1.ap[:, hs_], True, True, [hmlp, hm1], [ps])
                sin3(ps, frs.ap[:, 2:3], h2aug.ap[0:64, d_, hs_], h2aug)
        arena.reset(m2)

        def AB(which, ft):
            k = which * 4 + ft // 2
            return V(hb[k].ap.bitcast(F32)[:, (ft % 2) * 256:(ft % 2 + 1) * 256], hb[k].regs)

        z_tm = arena.alloc([128, NCH, 256], BF16)
        m3 = arena.mark()
        for o in range(2):
            arena.reset(m3)
            fsum = arena.alloc([128, 8, 256], BF16)
            fdif = arena.alloc([128, 8, 256], BF16)
            ftm2 = arena.alloc([128, 2, 256], F32)
            wtm2 = [arena.alloc([128, 2, 256], F32) for _ in range(2)]
            e4 = [arena.alloc([128, 256], F32) for _ in range(2)]
            for tt_ in range(8):
                ps = bank()
                w2 = wtm2[tt_ % 2]
                for d_ in range(2):
                    mm(ps.ap[:, d_ * 256:(d_ + 1) * 256], h2aug.ap[0:65, d_, tt_ * 128:(tt_ + 1) * 128],
                       w3b.ap[0:65, (d_ * 2 + o) * 256:(d_ * 2 + o + 1) * 256], True, True, [h2aug, w3b], [ps])
                    act(w2.ap[:, d_, :], hnd.ap, AF.Exp, [hnd], [w2], scale=HTN(d_, tt_))
                    act(w2.ap[:, d_, :], w2.ap[:, d_, :], AF.Identity, [w2, hsv, hsv05], [w2],
                        scale=hsv.ap[:, d_ * 8 + tt_: d_ * 8 + tt_ + 1], bias=hsv05.ap[:, d_ * 8 + tt_: d_ * 8 + tt_ + 1])
                tt("dve", ftm2.ap, ps.ap.rearrange("p (d c) -> p d c", d=2), w2.ap, ALU.mult, [ps, w2], [ftm2])
                tt("pool", fsum.ap[:, tt_, :], ftm2.ap[:, 0, :], ftm2.ap[:, 1, :], ALU.add, [ftm2], [fsum])
                tt("dve", fdif.ap[:, tt_, :], ftm2.ap[:, 1, :], ftm2.ap[:, 0, :], ALU.subtract, [ftm2], [fdif])
            for ft in range(8):
                ps = bank()
                for tt_ in range(8):
                    mm(ps.ap[:, 0:256], Cm.ap[:, tt_, ft * 128:(ft + 1) * 128], fsum.ap[:, tt_, :], tt_ == 0, tt_ == 7,
                       [Cm, fsum], [ps])
                for tt_ in range(8):
                    mm(ps.ap[:, 256:512], Sm.ap[:, tt_, ft * 128:(ft + 1) * 128], fdif.ap[:, tt_, :], tt_ == 0,
                       tt_ == 7, [Sm, fdif], [ps])
                Kr, Ks = ps.ap[:, 0:256], ps.ap[:, 256:512]
                A_, B_ = AB(0, ft), AB(1, ft)
                stt(e4[0].ap, Ks, HNSIN[:, ft:ft + 1], biasA.ap[:, o * 256:(o + 1) * 256], ALU.mult, ALU.add,
                    [ps, hsm, biasA], [e4[0]])
                stt(A_.ap, Kr, HCOS[:, ft:ft + 1], e4[0].ap, ALU.mult, ALU.add, [ps, hsm, e4[0]], [A_])
                ts("dve", e4[1].ap, Ks, HCOS[:, ft:ft + 1], None, ALU.mult, None, [ps, hsm], [e4[1]])
                stt(B_.ap, Kr, HSIN[:, ft:ft + 1], e4[1].ap, ALU.mult, ALU.add, [ps, hsm, e4[1]], [B_])
            arena.reset(m3)
            Pq = arena.alloc([128, 8, 256], BF16)
            Qq = arena.alloc([128, 8, 256], BF16)
            e4 = [arena.alloc([128, 2, 256], F32) for _ in range(4)]
            zin = hv_tm if o == 0 else z_tm
            for m_ in range(4):
                ps2 = pair()
                for f2 in range(2):
                    ft = m_ * 2 + f2
                    base = f2 * 512
                    for tt_ in range(8):
                        mm(ps2.ap[:, base:base + 256], Cm.ap[:, tt_, ft * 128:(ft + 1) * 128], zin.ap[:, tt_, :],
                           tt_ == 0, tt_ == 7, [Cm, zin], [ps2.regs[f2]])
                    for tt_ in range(8):
                        mm(ps2.ap[:, base + 256:base + 512], Sm.ap[:, tt_, ft * 128:(ft + 1) * 128], zin.ap[:, tt_, :],
                           tt_ == 0, tt_ == 7, [Sm, zin], [ps2.regs[f2]])
                z4 = ps2.ap.rearrange("p (f x c) -> p f x c", f=2, x=2)
                Zc, Zs = z4[:, :, 0, :], z4[:, :, 1, :]
                A2 = V(hb[m_].ap.bitcast(F32).rearrange("p (f c) -> p f c", f=2), hb[m_].regs)
                B2 = V(hb[4 + m_].ap.bitcast(F32).rearrange("p (f c) -> p f c", f=2), hb[4 + m_].regs)
                tt("dve", e4[0].ap, Zc, A2.ap, ALU.mult, [ps2, A2], [e4[0]])
                tt("dve", e4[1].ap, Zs, B2.ap, ALU.mult, [ps2, B2], [e4[1]])
                tt("pool", Pq.ap[:, m_ * 2:m_ * 2 + 2, :], e4[0].ap, e4[1].ap, ALU.add, [e4[0], e4[1]], [Pq])
                tt("dve", e4[2].ap, Zs, A2.ap, ALU.mult, [ps2, A2], [e4[2]])
                tt("dve", e4[3].ap, Zc, B2.ap, ALU.mult, [ps2, B2], [e4[3]])
                tt("pool", Qq.ap[:, m_ * 2:m_ * 2 + 2, :], e4[2].ap, e4[3].ap, ALU.subtract, [e4[2], e4[3]], [Qq])
            if o == 0:
                for tt_ in range(8):
                    ps = bank()
                    for ft in range(8):
                        mm(ps.ap[:, 0:256], Cm.ap[:, ft, tt_ * 128:(tt_ + 1) * 128], Pq.ap[:, ft, :], ft == 0, False,
                           [Cm, Pq], [ps])
                    for ft in range(8):
                        mm(ps.ap[:, 0:256], Sm.ap[:, ft, tt_ * 128:(tt_ + 1) * 128], Qq.ap[:, ft, :], False, ft == 7,
                           [Sm, Qq], [ps])
                    tt("dve", z_tm.ap[:, tt_, :], ps.ap[:, 0:256], hx1_tm.ap[:, tt_, :], ALU.mult, [ps, hx1_tm], [z_tm])
            else:
                for ct in range(2):
                    for half in range(2):
                        hs_ = slice(half * 512, (half + 1) * 512)
                        ps = bank()
                        for ft in range(8):
                            mm(ps.ap, Pq.ap[:, ft, ct * 128:(ct + 1) * 128], Cm.ap[:, ft, hs_], ft == 0, False,
                               [Cm, Pq], [ps])
                        for ft in range(8):
                            mm(ps.ap, Qq.ap[:, ft, ct * 128:(ct + 1) * 128], Sm.ap[:, ft, hs_], False, ft == 7,
                               [Sm, Qq], [ps])
                        tt("dve", mixT.ap[:, 3 + ct, hs_], ps.ap, hx2.ap[:, ct, hs_], ALU.mult, [ps, hx2],
                           [mixr(3 + ct, 4 + ct)])
        arena.reset(am)
        dump("mix_hy", mixr(3, 4), mixT.ap[:, 3, :], [128, T])

        if stop == 'hy':
            return
        am = arena.mark()
        yo = [arena.alloc([128, T], F32) for _ in range(8)]
        wov = wout_d[l].rearrange("(kt p) n -> p kt n", p=128)
        for pnl in range(4):
            s = ring_load(wov[:, :, pnl * 256:(pnl + 1) * 256], 8, 256)
            for j in range(2):
                dt_ = pnl * 2 + j
                for half in range(2):
                    hs_ = slice(half * 512, (half + 1) * 512)
                    ps = bank()
                    for kt in range(8):
                        mm(ps.ap, s.ap[:, kt, j * 128:(j + 1) * 128], mixT.ap[:, kt, hs_], kt == 0, kt == 7,
                           [s, mixr(kt, kt + 1)], [ps])
                    cp("act", yo[dt_].ap[:, hs_], ps.ap, [ps], [yo[dt_]])
        dump("mixout", yo[0], yo[0].ap, [128, T])
        post_norm_residual(yo, (lambda kt: gmt.ap[:, 8 + kt:9 + kt], gmt))
        arena.reset(am)
        dump("x1", x[0], x[0].ap, [128, T])

        if stop == 'wout':
            return
        pre_norm((lambda kt: gmt.ap[:, 16 + kt:17 + kt], gmt), (lambda kt: modt.ap[:, 24 + kt:25 + kt], modt))
        am = arena.mark()
        yo = [arena.alloc([128, T], F32) for _ in range(8)]
        m1 = arena.mark()
        gbuf = mixT
        cvs = [arena.alloc([128, T], F32) for _ in range(2)]
        gls = [arena.alloc([128, T], F32) for _ in range(2)]
        wuv = wup_d[l].rearrange("(kt p) n -> p kt n", p=128)
        wdv = wdown_d[l].rearrange("(ft p) n -> p ft n", p=128)
        for g in range(4):
            for j in range(4):
                sa = ring_load(wuv[:, :, g * 1024 + j * 256: g * 1024 + (j + 1) * 256], 8, 256)
                sb_ = ring_load(wuv[:, :, 4096 + g * 1024 + j * 256: 4096 + g * 1024 + (j + 1) * 256], 8, 256)
                for jj in range(2):
                    ft = j * 2 + jj
                    fidx = g * 8 + ft
                    psa = pair()
                    psb2 = pair()
                    for half in range(2):
                        hs_ = slice(half * 512, (half + 1) * 512)
                        for kt in range(8):
                            mm(psa.ap[:, hs_], sa.ap[:, kt, jj * 128:(jj + 1) * 128], hb[kt].ap[:, hs_], kt == 0,
                               kt == 7, [sa, hb[kt]], [psa.regs[half]])
                    for half in range(2):
                        hs_ = slice(half * 512, (half + 1) * 512)
                        for kt in range(8):
                            mm(psb2.ap[:, hs_], sb_.ap[:, kt, jj * 128:(jj + 1) * 128], hb[kt].ap[:, hs_], kt == 0,
                               kt == 7, [sb_, hb[kt]], [psb2.regs[half]])
                    cv = cvs[ft % 2]
                    gl = gls[ft % 2]
                    conv3(psa.ap, psa, cv, fidx, (80, 112, 144), 176 + fidx, 0, fidx)
                    act(gl.ap, cv.ap, AF.Gelu_apprx_tanh, [cv], [gl])
                    tt("dve", gbuf.ap[:, ft, :], gl.ap, psb2.ap, ALU.mult, [gl, psb2], [mixr(ft, ft + 1)])
            for pnl in range(4):
                s = ring_load(wdv[:, g * 8:(g + 1) * 8, pnl * 256:(pnl + 1) * 256], 8, 256)
                for j in range(2):
                    dt_ = pnl * 2 + j
                    for half in range(2):
                        hs_ = slice(half * 512, (half + 1) * 512)
                        ps = bank()
                        for ft in range(8):
                            mm(ps.ap, s.ap[:, ft, j * 128:(j + 1) * 128], gbuf.ap[:, ft, hs_], ft == 0, ft == 7,
                               [s, mixr(ft, ft + 1)], [ps])
                        if g == 0:
                            cp("act", yo[dt_].ap[:, hs_], ps.ap, [ps], [yo[dt_]])
                        else:
                            tt("dve", yo[dt_].ap[:, hs_], yo[dt_].ap[:, hs_], ps.ap, ALU.add, [yo[dt_], ps], [yo[dt_]])
        dump("ffn", yo[0], yo[0].ap, [128, T])
        arena.reset(m1)
        post_norm_residual(yo, (lambda kt: gmt.ap[:, 24 + kt:25 + kt], gmt))
        arena.reset(am)

    gates_p = P.sb("gates_p", [128, NCH, 16], F32)

    for l in range(nlayers):
        layer(l)

    fence = [P.sb(f"fence{i}", [1, 4], F32) for i in range(3)]
    psf = bank()
    mm(psf.ap[0:1, 0:1], onesb.ap[0:1, 0:1], onesb.ap[0:1, 0:1], True, True, [onesb], [psf])
    cp("act", fence[0].ap[:, 0:1], psf.ap[0:1, 0:1], [psf], [fence[0]])
    memset("dve", fence[1].ap, 0.0, [fence[1]])
    memset("pool", fence[2].ap, 0.0, [fence[2]])
    for k in range(8):
        P.op("sp", lambda e, k=k: e.dma_start(out=yT_d[k * 128:(k + 1) * 128, :], in_=x[k].ap),
             reads=[x[k]] + fence, dsem=_sem_for(st_sems, x[k], "st"))
    P.emit(final_dsems=list(st_sems.values()))
    build.info = dict(sb_bytes=P.sb_bytes, arena_peak=arena.peak, sig=P.sig_counts, nops=len(P.ops),
                      dbg_outs=dbg_outs)
    return nc


GRID_W = 64
ROPE_BASE = 10000.0
N_BANDS = 16
HY_W = 256


def _fm_cols(v):
    return np.ascontiguousarray(v.reshape(-1, 128).T)


def _const_tables(is_latent):
    f32 = np.float32
    L = 1024 if is_latent else 256
    S = T // L
    N = 2 * L
    t = np.arange(T)
    if is_latent:
        row = (t // GRID_W).astype(f32)
        col = (t % GRID_W).astype(f32)
        nf = (DH // 2) // 2
        freqs = (ROPE_BASE ** (-np.arange(nf, dtype=f32) / nf)).astype(f32)
        ang = np.concatenate([row[:, None] * freqs, col[:, None] * freqs], axis=-1).astype(f32)
        cos, sin = np.cos(ang), np.sin(ang)
    else:
        cos, sin = np.ones((T, 48), f32), np.zeros((T, 48), f32)
    cos2 = np.concatenate([cos, cos], -1)
    sin2 = np.concatenate([-sin, sin], -1)
    rope = np.stack([cos2, sin2], 0).reshape(2, NCH, 128, DH).transpose(2, 0, 1, 3).astype(f32)
    a = 2 * np.pi * np.outer(np.arange(L) + 0.5, np.arange(L) + 0.5) / N
    C1, S1 = np.cos(a), np.sin(a)
    Cm = np.zeros((T, T), f32)
    Smt = np.zeros((T, T), f32)
    for s in range(S):
        Cm[s * L:(s + 1) * L, s * L:(s + 1) * L] = C1
        Smt[s * L:(s + 1) * L, s * L:(s + 1) * L] = S1
    tau = np.arange(T) % L

    def feat(tv):
        tn = (tv / L).astype(f32)
        bands = np.linspace(1e-4, N_BANDS - 1, N_BANDS, dtype=f32)
        ang = (2.0 * math.pi * tn[:, None] * bands[None, :]).astype(f32)
        return np.concatenate([tn[:, None], np.cos(ang), np.sin(ang)], -1).astype(f32)

    hfeat = np.stack([feat(tau.astype(f32)).T, feat((tau + 1).astype(f32)).T], 1).astype(f32)
    th = np.pi * (tau + 0.5) / N
    deltas = np.abs(np.linspace(math.log(1e-2) / 1.5, math.log(1e-2) / 0.3, HY_W, dtype=f32))
    hsm = np.zeros((128, 40 + 256), f32)
    hsm[:, 0:8] = _fm_cols(np.cos(th).astype(f32))
    hsm[:, 8:16] = _fm_cols(np.sin(th).astype(f32))
    hsm[:, 16:24] = -hsm[:, 8:16]
    hsm[:, 24:32] = _fm_cols((tau / L).astype(f32))
    hsm[:, 32:40] = _fm_cols(((tau + 1) / L).astype(f32))
    hsm[:, 40:296] = -deltas[None, :]
    hsv = np.zeros((128, 16), f32)
    hsv[:, 0:8] = 2.0 / N
    hsv[:, 8:16] = _fm_cols((tau < L - 1).astype(f32) * (2.0 / N))
    return dict(rope=rope, dftC=Cm, dftS=Smt, hfeat=hfeat, hsm=hsm, hsvd=hsv), N


def _ctab():
    f32 = np.float32
    j = np.arange(128)[:, None].astype(f32)
    i = np.arange(128)[None, :].astype(f32)
    ct = np.zeros((128, NCT), f32)
    ct[:, 0:128] = np.maximum(i - j, 0)
    ct[:, 128:256] = np.maximum(j - i, 0)
    ct[:, 256:384] = (j <= i)
    ct[:, 384:512] = (j >= i)
    ct[:, 512:640] = np.broadcast_to(i + 1.0, (128, 128))
    ct[:, 640:768] = np.broadcast_to(128.0 - i, (128, 128))
    ct[:, 768:772] = np.broadcast_to(127.0 - j, (128, 4))
    ct[:, 772:776] = np.broadcast_to(j, (128, 4))
    return ct


def _prep(inputs):
    f32 = np.float32
    g = {k: np.asarray(v, dtype=f32) for k, v in inputs.items()}
    w_in = g["w_in"]
    rq, rk, rv, rg = [w_in[:, :, i * 384:(i + 1) * 384] for i in range(4)]
    hy = w_in[:, :, 1536:2304]
    mq, mk, mv, mo = [w_in[:, :, 2304 + i * 384: 2304 + (i + 1) * 384] for i in range(4)]
    gt = w_in[:, :, 3840:3856]
    w_in_tm = np.ascontiguousarray(np.concatenate([rq, gt, rk, rv, rg, mq, mk, mv, mo], axis=2))
    w_in_fm = np.ascontiguousarray(hy)
    lp = np.zeros((DEPTH, 128, NLP), f32)
    lrep = np.zeros((DEPTH, 128, NLR), f32)
    hmlp = np.zeros((DEPTH, 65, NHM), f32)
    hyb = np.zeros((DEPTH, 128, 512), f32)
    for l in range(DEPTH):
        lp[l, :, 0:8] = _fm_cols(g["norm_mix_pre"][l])
        lp[l, :, 8:16] = _fm_cols(g["norm_mix_post"][l])
        lp[l, :, 16:24] = _fm_cols(g["norm_ffn_pre"][l])
        lp[l, :, 24:32] = _fm_cols(g["norm_ffn_post"][l])
        lp[l, :, 32:80] = _fm_cols(g["b_mod"][l])
        for tap in range(3):
            lp[l, :, 80 + tap * 32: 112 + tap * 32] = _fm_cols(g["ffn_conv_w"][l, tap])
            lp[l, :, 208 + tap * 6: 214 + tap * 6] = _fm_cols(g["hy_conv_w"][l, tap])
        lp[l, :, 176:208] = _fm_cols(g["ffn_conv_b"][l])
        lp[l, :, 226:232] = _fm_cols(g["hy_conv_b"][l])
        lp[l, :, 232:235] = _fm_cols(g["ret_norm_g"][l])
        lp[l, :, 235:238] = _fm_cols(g["ml_norm_g"][l])
        lrep[l, :, 0:384] = g["ret_norm_g"][l][None]
        lrep[l, :, 384:768] = g["ml_norm_g"][l][None]
        hyb[l, :, :] = g["hy_bias"][l].reshape(-1)[None]
        lrep[l, :, 768:784] = g["ml_gate_bias"][l].reshape(-1)[None]
        lrep[l, :, 784:792] = g["ret_decay_logit"][l].reshape(-1)[None]
        hmlp[l, 0:33, 0:64] = g["hy_f_w1"][l]
        hmlp[l, 0:64, 64:128] = g["hy_f_w2"][l]
        hmlp[l, 0:64, 128:1152] = g["hy_f_w3"][l]
        hmlp[l, 64, 128:1152] = g["hy_f_b3"][l]
        hmlp[l, 0:64, 1152] = g["hy_sin_freq"][l]
        hmlp[l, 0:64, 1153] = g["hy_f_b1"][l]
        hmlp[l, 0:64, 1154] = g["hy_f_b2"][l]
    shared = dict(w_mod=g["w_mod"], w_in_tm=w_in_tm, w_in_fm=w_in_fm, w_out=g["w_out"], w_up=g["w_up"],
                  w_down=g["w_down"], lp=lp, lrep=lrep, hmlp=hmlp, hyb=hyb, ctab=_ctab())
    tabs = {False: _const_tables(False), True: _const_tables(True)}
    maps = []
    for core in range(8):
        is_lat = core >= 4
        ct, N = tabs[is_lat]
        m = dict(shared)
        m.update(ct)
        vecs = np.ones((128, NVEC), f32)
        if is_lat:
            b = (core - 4) % 2
            xT = g["x_sample"][b].T
            cvec = g["c"][b]
            vecs[:, 0] = 0.0
            sret = g["state_ret"][b].transpose(0, 3, 1, 2, 4)
            sc = g["state_mlstm_c"][b].transpose(0, 3, 1, 2, 4)
            sn = g["state_mlstm_n"][b].transpose(0, 3, 1, 2)[..., None]
            scn = np.concatenate([sc, sn], -1)
            smm = np.broadcast_to(g["state_mlstm_m"][b].reshape(DEPTH, 1, 8), (DEPTH, 128, 8))
        else:
            xT = g["x_prompt"][4 * core:4 * core + 4].reshape(T, D).T
            cvec = g["c_ctx"]
            vecs[:, 0] = 1.0
            for c in range(8):
                vecs[:, 2 + c] = 0.0 if c % 2 == 0 else 1.0
                vecs[:, 10 + c] = 0.0 if c % 2 == 1 else 1.0
            sret = np.zeros((DEPTH, DH, 2, 4, DH), f32)
            scn = np.zeros((DEPTH, DH, 2, 4, DH + 1), f32)
            smm = np.zeros((DEPTH, 128, 8), f32)
        vecs[:, 1] = 2.0 / N
        m.update(xT=xT, cfm=_fm_cols(cvec), vecs=vecs, sret=sret, scn=scn, smm=smm)
        maps.append({k: np.ascontiguousarray(v, dtype=f32) for k, v in m.items()})
    return maps


_CACHE = {}


def _assemble(r):
    f = np.float32
    yp = np.stack([r[k]["yT"].T.reshape(4, 256, D) for k in range(4)], 0).reshape(16, 256, D)
    ys = np.stack([r[4]["yT"].T, r[5]["yT"].T], 0)
    o_ret = np.concatenate([r[k]["o_ret"] for k in range(4)], 0)
    o_c = np.concatenate([r[k]["o_c"] for k in range(4)], 0)
    o_n = np.concatenate([r[k]["o_n"] for k in range(4)], 0)
    o_m = np.concatenate([r[k]["o_m"] for k in range(4)], 0)
    return (yp.astype(f), ys.astype(f), o_ret.astype(f), o_c.astype(f), o_n.astype(f), o_m.astype(f))


def kernel(**inputs):
    maps = _prep(inputs)
    if "nc" not in _CACHE:
        _CACHE["nc"] = build()
    res = run_bass_kernel_spmd(_CACHE["nc"], maps, core_ids=list(range(8)))
    return _assemble(res.results)
```
